# Optimizing a Trainium2 kernel written in Bass

```python
import math
import jax, jax.numpy as jnp
from jax import lax
import numpy as np

D_MODEL = 2048
BATCH = 4
SEQ = 4096
DEPTH = 4

N_A_LAYERS = DEPTH // 2
N_B_LAYERS = DEPTH - N_A_LAYERS
RMS_EPS = 1e-6

E_A = D_MODEL
POOL_WINDOWS = (2, 4, 8, 16)
N_POOL_GROUPS = len(POOL_WINDOWS)
POOL_GROUP_DIM = E_A // N_POOL_GROUPS

HEAD_DIM = 128
HEADS_PER_GROUP = D_MODEL // HEAD_DIM
DILATED_PAIRS = ((128, 1), (512, 4), (2048, 16))
N_DIL_GROUPS = len(DILATED_PAIRS)
E_B = HEADS_PER_GROUP * HEAD_DIM
ROPE_THETA = 10000.0
NEG_INF = -1e30

kernel_name = "yoco_pool_dilated_hybrid"


def rmsnorm(x, g):
    xf = x.astype(jnp.float32)
    inv = lax.rsqrt(jnp.mean(xf * xf, axis=-1, keepdims=True) + RMS_EPS)
    return (xf * inv).astype(x.dtype) * g


def rope_tables(seq):
    inv_freq = 1.0 / (ROPE_THETA ** (jnp.arange(0, HEAD_DIM, 2, dtype=jnp.float32) / HEAD_DIM))
    ang = jnp.arange(seq, dtype=jnp.float32)[:, None] * inv_freq[None, :]
    return jnp.cos(ang), jnp.sin(ang)


def apply_rope(t, cos, sin):
    tf = t.astype(jnp.float32)
    t1, t2 = tf[..., : HEAD_DIM // 2], tf[..., HEAD_DIM // 2:]
    c, s = cos[None, :, None, :], sin[None, :, None, :]
    return jnp.concatenate([t1 * c - t2 * s, t2 * c + t1 * s], axis=-1).astype(t.dtype)


def multiscale_causal_pool(u):
    b, s, _ = u.shape
    u4 = u.reshape(b, s, N_POOL_GROUPS, POOL_GROUP_DIM)
    csum = jnp.cumsum(u4.astype(jnp.float32), axis=1)
    csum = jnp.concatenate([jnp.zeros_like(csum[:, :1]), csum], axis=1)
    win = jnp.asarray(POOL_WINDOWS, dtype=jnp.int32)
    t1 = jnp.arange(1, s + 1, dtype=jnp.int32)[:, None]
    lower = jnp.maximum(t1 - win[None, :], 0)
    c_lo = csum[:, lower, jnp.arange(N_POOL_GROUPS)[None, :], :]
    count = jnp.minimum(t1, win[None, :]).astype(jnp.float32)
    mean = (csum[:, 1:] - c_lo) / count[None, :, :, None]
    return (mean - u4.astype(jnp.float32)).astype(u.dtype)


def dilated_window_attention(q, k, v, window, dilation):
    b, s, h, hd = q.shape
    d = dilation
    nb = window // dilation
    m = s // d
    nblk = -(-m // nb)
    m_pad = nblk * nb

    def residues(t):
        return t.reshape(b, m, d, h, hd).transpose(0, 2, 3, 1, 4)

    qr, kr, vr = residues(q), residues(k), residues(v)
    qb = jnp.pad(qr, ((0, 0), (0, 0), (0, 0), (0, m_pad - m), (0, 0))).reshape(b, d, h, nblk, nb, hd)
    kv_pad = ((0, 0), (0, 0), (0, 0), (nb, m_pad - m), (0, 0))
    kp = jnp.pad(kr, kv_pad).reshape(b, d, h, nblk + 1, nb, hd)
    vp = jnp.pad(vr, kv_pad).reshape(b, d, h, nblk + 1, nb, hd)
    kb = jnp.concatenate([kp[:, :, :, :-1], kp[:, :, :, 1:]], axis=4)
    vb = jnp.concatenate([vp[:, :, :, :-1], vp[:, :, :, 1:]], axis=4)

    scores = jnp.einsum('bdhnqc,bdhnkc->bdhnqk', qb, kb).astype(jnp.float32) * (1.0 / math.sqrt(hd))
    r_idx = jnp.arange(nb)[:, None]
    c_idx = jnp.arange(2 * nb)[None, :]
    band = (c_idx >= r_idx) & (c_idx <= r_idx + nb)
    blk = jnp.arange(nblk)[:, None, None]
    mask = band[None] & (blk * nb + c_idx[None] >= nb)
    scores = jnp.where(mask[None, None, None], scores, NEG_INF)
    lse = jax.nn.logsumexp(scores, axis=-1)
    p = jnp.exp(scores - lse[..., None]).astype(v.dtype)
    out = jnp.einsum('bdhnqk,bdhnkc->bdhnqc', p, vb)

    out = out.reshape(b, d, h, m_pad, hd)[:, :, :, :m]
    lse = lse.reshape(b, d, h, m_pad)[:, :, :, :m]
    out = out.transpose(0, 3, 1, 2, 4).reshape(b, s, h, hd)
    lse = lse.transpose(0, 3, 1, 2).reshape(b, s, h)
    return out, lse


def setup_inputs(seed: int = 0) -> dict:
    key = jax.random.key(seed)
    ks = jax.random.split(key, 13)
    f32 = jnp.float32
    x = jax.random.normal(ks[0], (BATCH, SEQ, D_MODEL), f32)
    norm_a = 1.0 + 0.1 * jax.random.normal(ks[1], (N_A_LAYERS, D_MODEL), f32)
    w_in_a = jax.random.normal(ks[2], (N_A_LAYERS, D_MODEL, 2 * E_A), f32) * D_MODEL ** -0.5
    w_grp_a = jax.random.normal(ks[3], (N_A_LAYERS, N_POOL_GROUPS, POOL_GROUP_DIM, POOL_GROUP_DIM), f32) * POOL_GROUP_DIM ** -0.5
    scale_a = 1.0 + 0.1 * jax.random.normal(ks[4], (N_A_LAYERS, E_A), f32)
    w_out_a = jax.random.normal(ks[5], (N_A_LAYERS, E_A, D_MODEL), f32) * E_A ** -0.5
    norm_kv = 1.0 + 0.1 * jax.random.normal(ks[6], (D_MODEL,), f32)
    w_k = jax.random.normal(ks[7], (D_MODEL, E_B), f32) * D_MODEL ** -0.5
    w_v = jax.random.normal(ks[8], (D_MODEL, E_B), f32) * D_MODEL ** -0.5
    norm_b = 1.0 + 0.1 * jax.random.normal(ks[9], (N_B_LAYERS, D_MODEL), f32)
    w_in_b = jax.random.normal(ks[10], (N_B_LAYERS, D_MODEL, N_DIL_GROUPS * E_B + E_B), f32) * D_MODEL ** -0.5
    w_out_b = jax.random.normal(ks[11], (N_B_LAYERS, E_B, D_MODEL), f32) * E_B ** -0.5
    norm_f = 1.0 + 0.1 * jax.random.normal(ks[12], (D_MODEL,), f32)
    return {"x": x, "norm_a": norm_a, "w_in_a": w_in_a, "w_grp_a": w_grp_a, "scale_a": scale_a,
            "w_out_a": w_out_a, "norm_kv": norm_kv, "w_k": w_k, "w_v": w_v, "norm_b": norm_b,
            "w_in_b": w_in_b, "w_out_b": w_out_b, "norm_f": norm_f}


def reference(x, norm_a, w_in_a, w_grp_a, scale_a, w_out_a, norm_kv, w_k, w_v, norm_b, w_in_b, w_out_b, norm_f):
    b, s, _ = x.shape
    cos, sin = rope_tables(s)
    k_shared = None
    v_shared = None
    for layer in range(DEPTH):
        if layer < N_A_LAYERS:
            i = layer
            hdn = rmsnorm(x, norm_a[i])
            proj = hdn @ w_in_a[i]
            u, gate = proj[..., :E_A], proj[..., E_A:]
            pooled = multiscale_causal_pool(u)
            y = jnp.einsum('bsgc,gcd->bsgd', pooled, w_grp_a[i]).reshape(b, s, E_A) * scale_a[i]
            x = x + (y * jax.nn.silu(gate)) @ w_out_a[i]
            if layer == N_A_LAYERS - 1:
                kv_in = rmsnorm(x, norm_kv)
                k_shared = apply_rope((kv_in @ w_k).reshape(b, s, HEADS_PER_GROUP, HEAD_DIM), cos, sin)
                v_shared = (kv_in @ w_v).reshape(b, s, HEADS_PER_GROUP, HEAD_DIM)
        else:
            i = layer - N_A_LAYERS
            hdn = rmsnorm(x, norm_b[i])
            proj = hdn @ w_in_b[i]
            q_all = proj[..., : N_DIL_GROUPS * E_B].reshape(b, s, N_DIL_GROUPS, HEADS_PER_GROUP, HEAD_DIM)
            gate = proj[..., N_DIL_GROUPS * E_B:]
            outs = []
            lses = []
            for g, (window, dilation) in enumerate(DILATED_PAIRS):
                q = apply_rope(q_all[:, :, g], cos, sin)
                o_g, lse_g = dilated_window_attention(q, k_shared, v_shared, window, dilation)
                outs.append(o_g)
                lses.append(lse_g)
            alpha = jax.nn.softmax(jnp.stack(lses, axis=0), axis=0)
            merged = jnp.sum(alpha[..., None].astype(x.dtype) * jnp.stack(outs, axis=0), axis=0)
            merged = merged.reshape(b, s, E_B)
            x = x + (merged * jax.nn.silu(gate)) @ w_out_b[i]
    return rmsnorm(x, norm_f)
```

```python
import math
import numpy as np
import ml_dtypes
import concourse.bass as bass
import concourse.mybir as mybir
from concourse.bass_utils import run_bass_kernel_spmd

F32 = mybir.dt.float32
BF16 = mybir.dt.bfloat16
AF = mybir.ActivationFunctionType
ALU = mybir.AluOpType

D = 2048
KC = 16
LA = 4096
LB = 2048
NTA = LA // 128
NTB = LB // 128
TB0 = NTA - NTB
EPS = 1e-6
POOLW = (2, 4, 8, 16)
DILS = (1, 4, 16)
HD = 128
NH = 16
SCALE = 1.0 / math.sqrt(HD)
PAGE = 256
NDMASEM = 12

KB = 1024
WA = 0
WB = 64 * KB
HS0 = 128 * KB
HS1 = 144 * KB
XR = 160 * KB
CR = 184 * KB
ARENA = 196 * KB


def _esz(dt):
    return 4 if dt == F32 else 2


class Prog:
    def __init__(self, nc):
        self.nc = nc
        self.ops = []

    def add(self, eng, fn, reads=(), writes=(), dr=(), dw=(), dma=False):
        self.ops.append(dict(eng=eng, fn=fn, reads=list(reads), writes=list(writes), dr=list(dr),
                             dw=list(dw), dma=dma, signal=False, waits=None))

    @staticmethod
    def keys(ap):
        dims = ap.ap
        pstride, pcount = dims[0]
        p0 = ap.base_partition()
        esz = _esz(ap.dtype)
        col = ap.offset - p0 * pstride
        name = ap.tensor.name
        page = 2048 if name == "PS" else PAGE
        big = [(s, n) for s, n in dims[1:] if n > 1 and s * esz >= page]
        small = [(s, n) for s, n in dims[1:] if not (n > 1 and s * esz >= page)]
        offsets = [0]
        for s, n in big:
            if len(offsets) * n > 512:
                small.append((s, n))
                continue
            offsets = [o + i * s for o in offsets for i in range(n)]
        span = sum((n - 1) * s for s, n in small) + 1
        pages = set()
        for o in offsets:
            lo_b = (col + o) * esz
            hi_b = (col + o + span) * esz
            pages.update(range(lo_b // page, (hi_b - 1) // page + 1))
        q0, q1 = p0 // 32, (p0 + pcount - 1) // 32
        return [(name, pg, q) for pg in pages for q in range(q0, q1 + 1)]

    def analyze(self, sems, dma_sems):
        lastw = {}
        readers = {}
        eng_sig = {}
        dma_cnt = {}
        ops = self.ops
        for i, op in enumerate(ops):
            deps = set()
            rk = []
            wk = []
            for ap in op["reads"]:
                rk.extend(self.keys(ap))
            for k in op["dr"]:
                rk.append(("dram", k))
            for ap in op["writes"]:
                wk.extend(self.keys(ap))
            for k in op["dw"]:
                wk.append(("dram", k))
            for k in rk:
                w = lastw.get(k)
                if w is not None:
                    deps.add(w)
                if k[0] == "PS":
                    r = readers.get(k)
                    if r:
                        for en, j in r.items():
                            if en != op["eng"]:
                                deps.add(j)
            for k in wk:
                w = lastw.get(k)
                if w is not None:
                    deps.add(w)
                r = readers.get(k)
                if r:
                    deps.update(r.values())
            for k in rk:
                readers.setdefault(k, {})[op["eng"]] = i
            for k in wk:
                lastw[k] = i
                readers[k] = {}
            deps.discard(i)
            op["deps"] = deps
            for j in deps:
                oj = ops[j]
                if oj["eng"] == "pe" and op["eng"] == "pe" and not oj["dma"] and not op["dma"]:
                    continue
                oj["signal"] = True
        for i, op in enumerate(ops):
            e = op["eng"]
            if op["dma"]:
                k = dma_cnt.get(e, 0)
                dma_cnt[e] = k + 1
                nsem = 3 if e == "pool" else NDMASEM
                sem = dma_sems[e][k % nsem]
                op["sem"] = sem
                op["cnt"] = 16 * (k // nsem + 1)
                op["prev"] = 16 * (k // nsem)
                op["signal"] = True
            elif op["signal"]:
                c = eng_sig.get(e, 0) + 1
                eng_sig[e] = c
                op["sem"] = sems[e]
                op["cnt"] = c
        waited = {}
        for i, op in enumerate(ops):
            e = op["eng"]
            need = {}
            for j in op["deps"]:
                oj = ops[j]
                if oj["eng"] == "pe" and e == "pe" and not oj["dma"] and not op["dma"]:
                    continue
                s = oj["sem"]
                key = id(s)
                if key not in need or need[key][1] < oj["cnt"]:
                    need[key] = (s, oj["cnt"])
            if op["dma"] and op["prev"] > 0:
                s = op["sem"]
                key = id(s)
                if key not in need or need[key][1] < op["prev"]:
                    need[key] = (s, op["prev"])
            w = []
            wd = waited.setdefault(e, {})
            for key, (s, c) in need.items():
                if wd.get(key, 0) >= c:
                    continue
                wd[key] = c
                w.append((s, c))
            op["waits"] = w
        self.final_dma = {}
        for op in ops:
            if op["dma"]:
                self.final_dma[id(op["sem"])] = (op["sem"], op["cnt"])

    def emit(self, engname, eng):
        for op in self.ops:
            if op["eng"] != engname:
                continue
            waits = op["waits"]
            if engname == "pe" and waits:
                for s, c in waits[1:]:
                    eng.wait_ge(s, c)
                ins = op["fn"](eng)
                ins._wait_ge(waits[0][0], waits[0][1])
            else:
                for s, c in waits:
                    eng.wait_ge(s, c)
                ins = op["fn"](eng)
            if op["signal"]:
                ins.then_inc(op["sem"], 16 if op["dma"] else 1)


def build_program(debug=False):
    nc = bass.Bass("TRN2", target_bir_lowering=False)
    dt = nc.dram_tensor
    x_in = dt("x_in", [LA, D], F32, kind="ExternalInput").ap()
    w_in_a = dt("w_in_a", [2, D, 2 * D], F32, kind="ExternalInput").ap()
    w_grp_a = dt("w_grp_a", [2, 4, 512, 512], F32, kind="ExternalInput").ap()
    w_out_a = dt("w_out_a", [2, D, D], F32, kind="ExternalInput").ap()
    w_k = dt("w_k", [D, D], F32, kind="ExternalInput").ap()
    w_v = dt("w_v", [D, D], F32, kind="ExternalInput").ap()
    w_in_b = dt("w_in_b", [2, D, 4 * D], F32, kind="ExternalInput").ap()
    w_out_b = dt("w_out_b", [2, D, D], F32, kind="ExternalInput").ap()
    gains_d = dt("gains", [128, 5 * 16], F32, kind="ExternalInput").ap()
    gf_d = dt("gf", [1, D], F32, kind="ExternalInput").ap()
    scale_d = dt("scale_fm", [128, 2 * 16], F32, kind="ExternalInput").ap()
    invcnt_d = dt("invcnt", [128, 2 * 4 * 16], F32, kind="ExternalInput").ap()
    cos_d = dt("cosT", [128, LA], F32, kind="ExternalInput").ap()
    sin_d = dt("sinT", [128, LA], F32, kind="ExternalInput").ap()
    masks_d = dt("masks", [128, 3 * 512], BF16, kind="ExternalInput").ap()
    out_d = dt("out", [LB, D], F32, kind="ExternalOutput").ap()
    skind = "ExternalOutput" if debug else "Internal"
    xsA = dt("xsA", [LA, D], F32, kind=skind).ap()
    xsB = dt("xsB", [LA, D], F32, kind=skind).ap()
    hT = dt("hT", [NTA, 128, KC, 128], BF16, kind=skind).ap()
    hTkv = dt("hTkv", [NTA, 128, KC, 128], BF16, kind=skind).ap()
    mT = dt("mT", [NTA, 128, KC, 128], BF16, kind=skind).ap()
    KTd = dt("KTd", [NH, 128, LA], BF16, kind=skind).ap()
    Vd = dt("Vd", [LA, D], BF16, kind=skind).ap()

    P = Prog(nc)

    with (nc.sbuf_tensor("SB", [128, ARENA // 4], F32) as SBT,
          nc.psum_tensor("PS", [128, 4096], F32) as PST):

        def sb(off, shape, dtp=BF16):
            esz = _esz(dtp)
            n = int(np.prod(shape))
            assert off % 4 == 0 and (n * esz) % 4 == 0
            ap = SBT[:, off // 4:(off + n * esz) // 4]
            if dtp != F32:
                ap = ap.bitcast(dtp)
            if len(shape) == 2:
                ap = ap.rearrange("p (a b) -> p a b", b=shape[1])
            elif len(shape) == 3:
                ap = ap.rearrange("p (a b c) -> p a b c", b=shape[1], c=shape[2])
            return ap

        def psb(bank, nbanks=1, dtp=F32):
            ap = PST[:, bank * 512:(bank + nbanks) * 512]
            if dtp != F32:
                ap = ap.bitcast(dtp)
            return ap

        c_ident = sb(CR + 0, [128], BF16)
        c_ones = sb(CR + 256, [128], BF16)
        c_gains = sb(CR + 512, [5, 16], F32)
        c_scale = sb(CR + 832, [2, 16], F32)
        c_invcnt = sb(CR + 960, [2, 4, 16], F32)
        c_eps = sb(CR + 1472, [1], F32)
        c_identf = sb(CR + 1536, [128], F32)
        c_stats = sb(CR + 2048, [4, 4], F32)
        c_masks = sb(CR + 2560, [3, 512], BF16)

        def dma(eng, out, in_, reads=(), writes=(), dr=(), dw=()):
            P.add(eng, lambda e, o=out, i=in_: e.dma_start(out=o, in_=i), reads=reads, writes=writes,
                  dr=dr, dw=dw, dma=True)

        dma("sp", c_masks, masks_d.rearrange("p (a b) -> p a b", b=512), writes=[c_masks])
        dma("sp", c_gains, gains_d.rearrange("p (a b) -> p a b", b=16), writes=[c_gains])
        dma("sp", c_scale, scale_d.rearrange("p (a b) -> p a b", b=16), writes=[c_scale])
        dma("sp", c_invcnt, invcnt_d.rearrange("p (a b c) -> p a b c", b=4, c=16), writes=[c_invcnt])
        P.add("pool", lambda e: e.memset(c_identf, 0.0), writes=[c_identf])
        P.add("pool", lambda e: e.affine_select(out=c_identf, in_=c_identf, pattern=[[-1, 128]],
                                                compare_op=ALU.not_equal, fill=1.0, base=0,
                                                channel_multiplier=1),
              reads=[c_identf], writes=[c_identf])
        P.add("dve", lambda e: e.tensor_copy(out=c_ident, in_=c_identf), reads=[c_identf], writes=[c_ident])
        P.add("dve", lambda e: e.memset(c_ones, 1.0), writes=[c_ones])
        P.add("dve", lambda e: e.memset(c_eps, EPS), writes=[c_eps])

        bank_rr = [0]

        def next_bank(lo=0, hi=8):
            b = bank_rr[0]
            if b < lo or b >= hi:
                b = lo
            bank_rr[0] = b + 1 if b + 1 < hi else lo
            return b

        def load_w(dst, src):
            n = dst.shape[2]
            s3 = src.rearrange("(kc p) n -> p kc n", p=128)
            step = 512
            for c0 in range(0, n, step):
                c1 = min(n, c0 + step)
                dma("pool", dst[:, :, c0:c1], s3[:, :, c0:c1], writes=[dst[:, :, c0:c1]])

        def normA(xt, xn, si, need_xn=True):
            ssq = c_stats[:, si, 0:1]
            std = c_stats[:, si, 1:2]
            rstd = c_stats[:, si, 2:3]
            P.add("act", lambda e: e.activation(out=xn, in_=xt, func=AF.Square, accum_out=ssq),
                  reads=[xt], writes=[xn, ssq])
            P.add("act", lambda e: e.activation(out=std, in_=ssq, func=AF.Sqrt, bias=c_eps, scale=1.0 / D),
                  reads=[ssq, c_eps], writes=[std])
            P.add("dve", lambda e: e.reciprocal(out=rstd, in_=std), reads=[std], writes=[rstd])
            if need_xn:
                P.add("act", lambda e: e.activation(out=xn, in_=xt, func=AF.Copy, scale=rstd),
                      reads=[xt, rstd], writes=[xn])
            return rstd

        def normB(xn, pst, outs):
            for c in range(KC):
                P.add("pe", lambda e, c=c: e.transpose(out=pst[:, c, :], in_=xn[:, c * 128:(c + 1) * 128],
                                                       identity=c_ident),
                      reads=[xn[:, c * 128:(c + 1) * 128], c_ident], writes=[pst[:, c, :]])
            for (gi, hto, dst, key) in outs:
                g = c_gains[:, gi, :].unsqueeze(2).to_broadcast([128, KC, 128])
                P.add("dve", lambda e, hto=hto, g=g: e.tensor_tensor(out=hto, in0=pst, in1=g, op=ALU.mult),
                      reads=[pst, c_gains], writes=[hto])
                dma("sp", dst, hto, reads=[hto], dw=[key])

        def pst_view(bank):
            return psb(bank, 2, BF16).rearrange("p (c k) -> p c k", k=128)

        def hto_buf(i):
            return sb(HS0 + 8 * KB + i * 4 * KB, [KC, 128], BF16)

        def stage_norm0():
            xts = [sb(WB + i * 8 * KB, [D], F32) for i in range(4)]
            xns = [sb(WB + 32 * KB + i * 4 * KB, [D], BF16) for i in range(4)]
            htos = [sb(WB + 48 * KB + i * 4 * KB, [KC, 128], BF16) for i in range(4)]
            psts = [pst_view(4), pst_view(6)]

            def load(T):
                dma("sp", xts[T % 4], x_in[T * 128:(T + 1) * 128, :], writes=[xts[T % 4]])

            def partB(T):
                normB(xns[T % 4], psts[T % 2], [(0, htos[T % 4], hT[T], ("hT", T))])

            load(0)
            load(1)
            for T in range(NTA):
                if T + 2 < NTA:
                    load(T + 2)
                normA(xts[T % 4], xns[T % 4], T % 4)
                if T >= 1:
                    partB(T - 1)
            partB(NTA - 1)

        def stage_ag(l, g, wbase, first_loaded=False, prefetch_next=False):
            Wu = sb(wbase, [KC, 512], BF16)
            Wg = sb(wbase + 16 * KB, [KC, 512], BF16)
            Wgrp = sb(wbase + 32 * KB, [4, 512], BF16)
            load_w(Wu, w_in_a[l][:, g * 512:(g + 1) * 512])
            load_w(Wg, w_in_a[l][:, D + g * 512:D + (g + 1) * 512])
            load_w(Wgrp, w_grp_a[l][g])
            U = [sb(WA + 36 * KB + i * 8448, [4, 528], F32) for i in range(2)]
            T1 = sb(WA + 36 * KB + 2 * 8448, [4, 528], F32)
            T2 = sb(WB + 36 * KB, [4, 528], F32)
            sg = sb(WB + 36 * KB + 8448, [4, 512], BF16)
            pooled = sb(WB + 36 * KB + 8448 + 4096, [4, 512], BF16)
            mbuf = [sb(XR + i * 4 * KB, [4, 4, 128], BF16) for i in range(2)]
            hsb = [sb(HS0, [4, KC, 128], BF16), sb(HS1, [4, KC, 128], BF16)]
            w = POOLW[g]
            nsb = NTA // 4

            def load(s):
                h = hsb[s % 2]
                src = hT[4 * s:4 * s + 4].rearrange("t p c k -> p t (c k)")
                dma("sp", h.rearrange("p t c k -> p t (c k)"), src, writes=[h],
                    dr=[("hT", 4 * s + i) for i in range(4)])

            def ug(s):
                h = hsb[s % 2]
                u = U[s % 2]
                for j in range(4):
                    b = next_bank(0, 6)
                    ps = psb(b)
                    for kc in range(KC):
                        P.add("pe", lambda e, ps=ps, kc=kc, j=j, h=h: e.matmul(
                            ps, lhsT=Wu[:, kc, j * 128:(j + 1) * 128], rhs=h[:, :, kc, :],
                            start=(kc == 0), stop=(kc == KC - 1)),
                            reads=[Wu[:, kc, j * 128:(j + 1) * 128], h[:, :, kc, :]], writes=[ps])
                    P.add("act", lambda e, ps=ps, u=u, j=j: e.activation(out=u[:, j, 16:528], in_=ps, func=AF.Copy),
                          reads=[ps], writes=[u[:, j, 16:528]])
                if s == 0:
                    P.add("dve", lambda e, u=u: e.memset(u[:, :, 0:16], 0.0), writes=[u[:, :, 0:16]])
                else:
                    up = U[(s - 1) % 2]
                    P.add("dve", lambda e, u=u, up=up: e.tensor_copy(out=u[:, :, 0:16], in_=up[:, :, 512:528]),
                          reads=[up[:, :, 512:528]], writes=[u[:, :, 0:16]])

            def gate(s):
                h = hsb[s % 2]
                for j in range(4):
                    b = next_bank(0, 6)
                    ps = psb(b)
                    for kc in range(KC):
                        P.add("pe", lambda e, ps=ps, kc=kc, j=j, h=h: e.matmul(
                            ps, lhsT=Wg[:, kc, j * 128:(j + 1) * 128], rhs=h[:, :, kc, :],
                            start=(kc == 0), stop=(kc == KC - 1)),
                            reads=[Wg[:, kc, j * 128:(j + 1) * 128], h[:, :, kc, :]], writes=[ps])
                    P.add("act", lambda e, ps=ps, j=j: e.activation(out=sg[:, j, :], in_=ps, func=AF.Silu),
                          reads=[ps], writes=[sg[:, j, :]])

            def pool(s):
                u = U[s % 2]
                src = u
                bufs = [T1, T2]
                sh = 1
                lvl = 0
                while sh < w:
                    dst = bufs[lvl % 2]
                    lo = 2 * sh - 1
                    P.add("dve", lambda e, dst=dst, src=src, lo=lo, sh=sh: e.tensor_tensor(
                        out=dst[:, :, lo:528], in0=src[:, :, lo:528], in1=src[:, :, lo - sh:528 - sh], op=ALU.add),
                        reads=[src], writes=[dst[:, :, lo:528]])
                    src = dst
                    sh *= 2
                    lvl += 1
                P.add("dve", lambda e, src=src, u=u: e.scalar_tensor_tensor(
                    out=pooled, in0=src[:, :, 16:528], scalar=1.0 / w, in1=u[:, :, 16:528],
                    op0=ALU.mult, op1=ALU.subtract),
                    reads=[src, u], writes=[pooled])
                if s in (0, nsb // 2):
                    loc = 0 if s == 0 else 1
                    ic = c_invcnt[:, loc, g, :].unsqueeze(1).to_broadcast([128, 4, 16])
                    tmp = T1 if src is T2 else T2
                    P.add("dve", lambda e, src=src, ic=ic, tmp=tmp: e.tensor_tensor(
                        out=tmp[:, :, 0:16], in0=src[:, :, 16:32], in1=ic, op=ALU.mult),
                        reads=[src, c_invcnt], writes=[tmp[:, :, 0:16]])
                    P.add("dve", lambda e, tmp=tmp, u=u: e.tensor_tensor(
                        out=pooled[:, :, 0:16], in0=tmp[:, :, 0:16], in1=u[:, :, 16:32], op=ALU.subtract),
                        reads=[tmp[:, :, 0:16], u], writes=[pooled[:, :, 0:16]])

            def ymm(s):
                mb = mbuf[s % 2]
                for jo in range(4):
                    b = next_bank(0, 6)
                    ps = psb(b)
                    for ji in range(4):
                        P.add("pe", lambda e, ps=ps, ji=ji, jo=jo: e.matmul(
                            ps, lhsT=Wgrp[:, ji, jo * 128:(jo + 1) * 128], rhs=pooled[:, ji, :],
                            start=(ji == 0), stop=(ji == 3)),
                            reads=[Wgrp[:, ji, jo * 128:(jo + 1) * 128], pooled[:, ji, :]], writes=[ps])
                    sc = c_scale[:, l, g * 4 + jo:g * 4 + jo + 1]
                    P.add("dve", lambda e, ps=ps, mb=mb, jo=jo, sc=sc: e.scalar_tensor_tensor(
                        out=mb[:, :, jo, :], in0=ps.rearrange("p (t k) -> p t k", k=128), scalar=sc,
                        in1=sg[:, jo, :].rearrange("p (t k) -> p t k", k=128), op0=ALU.mult, op1=ALU.mult),
                        reads=[ps, sc, sg[:, jo, :]], writes=[mb[:, :, jo, :]])
                dst = mT[4 * s:4 * s + 4, :, 4 * g:4 * g + 4, :].rearrange("t p c k -> p t (c k)")
                dma("sp", dst, mb.rearrange("p t c k -> p t (c k)"), reads=[mb],
                    dw=[("mT", 4 * s + i, g) for i in range(4)])

            if not first_loaded:
                load(0)
            ug(0)
            gate(0)
            for s in range(nsb):
                if s + 1 < nsb:
                    load(s + 1)
                elif prefetch_next:
                    load(0)
                pool(s)
                if s + 1 < nsb:
                    ug(s + 1)
                ymm(s)
                if s + 1 < nsb:
                    gate(s + 1)

        def stage_ao(wsrc, wbase, xsrc, xdst, tiles, norm_outs, final=False, xkeysrc=None, xkeydst=None):
            Wo = sb(wbase, [KC, D], BF16)
            load_w(Wo, wsrc)
            if final:
                gfb = sb(HS1, [D], F32)
                dma("sp", gfb, gf_d[0:1, :].partition_broadcast(128), writes=[gfb])

            def load(T):
                par = T % 2
                xt = sb(XR + par * 8 * KB, [D], F32)
                mt = sb(HS0 + par * 4 * KB, [KC, 128], BF16)
                dma("sp", xt, xsrc[T * 128:(T + 1) * 128, :], writes=[xt],
                    dr=([(xkeysrc, T)] if xkeysrc else []))
                dma("sp", mt.rearrange("p c k -> p (c k)"), mT[T].rearrange("p c k -> p (c k)"), writes=[mt],
                    dr=[("mT", T, g) for g in range(4)] + [("mTh", T, h) for h in range(NH)])

            def comp(T):
                par = T % 2
                xt = sb(XR + par * 8 * KB, [D], F32)
                mt = sb(HS0 + par * 4 * KB, [KC, 128], BF16)
                xn = sb(XR + 16 * KB + par * 4 * KB, [D], BF16)
                for n in range(4):
                    b = next_bank(0, 6)
                    ps = psb(b)
                    for kc in range(KC):
                        P.add("pe", lambda e, ps=ps, kc=kc, n=n, mt=mt: e.matmul(
                            ps, lhsT=mt[:, kc, :], rhs=Wo[:, kc, n * 512:(n + 1) * 512],
                            start=(kc == 0), stop=(kc == KC - 1)),
                            reads=[mt[:, kc, :], Wo[:, kc, n * 512:(n + 1) * 512]], writes=[ps])
                    xs = xt[:, n * 512:(n + 1) * 512]
                    P.add("dve", lambda e, ps=ps, xs=xs: e.tensor_tensor(out=xs, in0=ps, in1=xs, op=ALU.add),
                          reads=[ps, xs], writes=[xs])
                if xdst is not None:
                    dma("sp", xdst[T * 128:(T + 1) * 128, :], xt, reads=[xt], dw=[(xkeydst, T)])
                outs = []
                for i, (gi, dstT, keyname, tmin) in enumerate(norm_outs):
                    if T >= tmin:
                        outs.append((gi, hto_buf(2 * i + par), dstT[T], (keyname, T)))
                rstd = normA(xt, xn, par, need_xn=bool(outs))
                if final:
                    ot = sb(HS1 + 8 * KB, [D], F32)
                    P.add("dve", lambda e, ot=ot, xt=xt, rstd=rstd: e.scalar_tensor_tensor(
                        out=ot, in0=xt, scalar=rstd, in1=gfb, op0=ALU.mult, op1=ALU.mult),
                        reads=[xt, rstd, gfb], writes=[ot])
                    dma("sp", out_d[(T - TB0) * 128:(T - TB0 + 1) * 128, :], ot, reads=[ot])
                return (xn, outs)

            load(tiles[0])
            prev = None
            for i, T in enumerate(tiles):
                if i + 1 < len(tiles):
                    load(tiles[i + 1])
                cur = comp(T)
                if prev is not None and prev[1]:
                    normB(prev[0], pst_view(6), prev[1])
                prev = cur
            if prev is not None and prev[1]:
                normB(prev[0], pst_view(6), prev[1])

        def rope(ps, cs_c, cs_s, dst, par):
            t1 = sb(XR + 16 * KB + par * 4 * KB, [512], F32)
            t2 = sb(XR + 16 * KB + 2 * KB + par * 4 * KB, [512], F32)
            P.add("dve", lambda e: e.tensor_tensor(out=t1, in0=ps, in1=cs_c, op=ALU.mult),
                  reads=[ps, cs_c], writes=[t1])
            P.add("dve", lambda e: e.tensor_tensor(out=t2[0:64, :], in0=ps[64:128, :], in1=cs_s[0:64, :], op=ALU.mult),
                  reads=[ps[64:128, :], cs_s[0:64, :]], writes=[t2[0:64, :]])
            P.add("dve", lambda e: e.tensor_tensor(out=t2[64:128, :], in0=ps[0:64, :], in1=cs_s[64:128, :], op=ALU.mult),
                  reads=[ps[0:64, :], cs_s[64:128, :]], writes=[t2[64:128, :]])
            P.add("dve", lambda e: e.tensor_tensor(out=dst, in0=t1, in1=t2, op=ALU.add),
                  reads=[t1, t2], writes=[dst])

        def load_cs(s_tok0, par):
            cc = sb(XR + par * 4 * KB, [512], F32)
            ss = sb(XR + 2 * KB + par * 4 * KB, [512], F32)
            dma("sp", cc, cos_d[:, s_tok0:s_tok0 + 512], writes=[cc])
            dma("sp", ss, sin_d[:, s_tok0:s_tok0 + 512], writes=[ss])
            return cc, ss

        def load_hsb(src_tiles, s4, par, keyname):
            h = sb(HS0 if par == 0 else HS1, [4, KC, 128], BF16)
            src = src_tiles[s4:s4 + 4].rearrange("t p c k -> p t (c k)")
            dma("sp", h.rearrange("p t c k -> p t (c k)"), src, writes=[h],
                dr=[(keyname, s4 + i) for i in range(4)])
            return h

        def stage_k(wbase):
            Wk = sb(wbase, [KC, D], BF16)
            load_w(Wk, w_k)
            nsb = NTA // 4
            cnt = [0]
            pre_k = {0: (load_hsb(hTkv, 0, 0, "hTkv"), load_cs(0, 0))}
            for s in range(nsb):
                h, (cc, ss) = pre_k.pop(s)
                for hd in range(NH):
                    if hd == 4 and s + 1 < nsb:
                        pre_k[s + 1] = (load_hsb(hTkv, 4 * (s + 1), (s + 1) % 2, "hTkv"),
                                        load_cs((s + 1) * 512, (s + 1) % 2))
                    b = next_bank(0, 8)
                    ps = psb(b)
                    for kc in range(KC):
                        P.add("pe", lambda e, ps=ps, kc=kc, hd=hd, h=h: e.matmul(
                            ps, lhsT=Wk[:, kc, hd * 128:(hd + 1) * 128], rhs=h[:, :, kc, :],
                            start=(kc == 0), stop=(kc == KC - 1)),
                            reads=[Wk[:, kc, hd * 128:(hd + 1) * 128], h[:, :, kc, :]], writes=[ps])
                    par = cnt[0] % 2
                    cnt[0] += 1
                    ko = sb(XR + 8 * KB + par * KB, [512], BF16)
                    rope(ps, cc, ss, ko, par)
                    dma("sp", KTd[hd][:, s * 512:(s + 1) * 512], ko, reads=[ko], dw=[("KT", hd, s)])

        def stage_v(wbase, vtalt):
            Wv = sb(wbase, [KC, D], BF16)
            load_w(Wv, w_v)
            nsb = NTA // 4
            vts = [sb(XR, [4, D], BF16), sb(vtalt, [4, D], BF16)]
            pre_v = {0: load_hsb(hTkv, 0, 0, "hTkv")}
            for s in range(nsb):
                h = pre_v.pop(s)
                vt = vts[s % 2]
                for tt in range(4):
                    if tt == 1 and s + 1 < nsb:
                        pre_v[s + 1] = load_hsb(hTkv, 4 * (s + 1), (s + 1) % 2, "hTkv")
                    for n in range(4):
                        b = next_bank(0, 8)
                        ps = psb(b)
                        for kc in range(KC):
                            P.add("pe", lambda e, ps=ps, kc=kc, n=n, tt=tt, h=h: e.matmul(
                                ps, lhsT=h[:, tt, kc, :], rhs=Wv[:, kc, n * 512:(n + 1) * 512],
                                start=(kc == 0), stop=(kc == KC - 1)),
                                reads=[h[:, tt, kc, :], Wv[:, kc, n * 512:(n + 1) * 512]], writes=[ps])
                        P.add("act", lambda e, ps=ps, vt=vt, tt=tt, n=n: e.activation(
                            out=vt[:, tt, n * 512:(n + 1) * 512], in_=ps, func=AF.Copy),
                            reads=[ps], writes=[vt[:, tt, n * 512:(n + 1) * 512]])
                dst = Vd[s * 512:(s + 1) * 512, :].rearrange("(t p) c -> p t c", p=128)
                dma("sp", dst, vt, reads=[vt], dw=[("V", s)])

        vblocks = {}
        vb_list = []
        for gi, d in enumerate(DILS):
            nq0 = LB // (128 * d)
            nblk = LA // (128 * d)
            if d == 16:
                order = [(r, n) for n in range(nq0 - 1, nblk) for r in range(d)]
            else:
                order = [(r, n) for r in range(d) for n in range(nq0 - 1, nblk)]
            for (r, n) in order:
                vblocks[(gi, r, n)] = len(vb_list)
                vb_list.append((gi, r, n))
        NVB = len(vb_list)

        def stage_bh(l):
            wh = [sb(WA + i * 16 * KB, [KC, 4, 128], BF16) for i in range(2)]
            KT = sb(WA + 32 * KB, [LA], BF16)
            qTs = [sb(WA + 40 * KB, [3, LB], BF16), sb(WB + 48 * KB, [3, LB], BF16)]
            sgBs = [sb(WA + 52 * KB, [LB], BF16), sb(WB + 60 * KB, [LB], BF16), sb(190 * KB, [LB], BF16)]
            mh = sb(WB + 18 * KB, [NTB, 128], BF16)
            ET = [sb(XR + 8 * KB + i * KB, [512], BF16) for i in range(3)]
            EM = [sb(XR + 11 * KB + i * KB, [512], BF16) for i in range(3)]
            Vh = sb(WB, [NVB, 128], BF16)
            acc_full = [sb(WB + 24 * KB, [2, LB], F32),
                        SBT[:, (WA + 56 * KB) // 4:(WA + 56 * KB) // 4 + 2 * 12288].rearrange("p (a s) -> p a s", a=2)[:, :, 0:LB]]
            acc_parts = [(sb(WB + 24 * KB, [LB], F32), sb(WB + 32 * KB, [LB], F32)),
                         (sb(WA + 56 * KB, [LB], F32), sb(WB + 40 * KB, [LB], F32))]
            pSs = [psb(2 + i) for i in range(3)]
            pOs = [psb(5 + i) for i in range(3)]

            def load_wh(hd):
                w = wh[hd % 2]
                for gq in range(4):
                    src = w_in_b[l][:, gq * D + hd * 128:gq * D + (hd + 1) * 128].rearrange("(kc p) n -> p kc n", p=128)
                    dma("pool", w[:, :, gq, :], src, writes=[w[:, :, gq, :]])

            def load_kt(hd):
                dma("sp", KT, KTd[hd], writes=[KT], dr=[("KT", hd, s) for s in range(NTA // 4)])

            def load_v(hd, gi):
                d = DILS[gi]
                nq0 = LB // (128 * d)
                nblk = LA // (128 * d)
                if d == 16:
                    for n in range(nq0 - 1, nblk):
                        src = Vd[:, hd * 128:(hd + 1) * 128].rearrange("(n i dd) c -> i n dd c", i=128, dd=d)[:, n, :, :]
                        i0 = vblocks[(gi, 0, n)]
                        dst = Vh[:, i0:i0 + d, :]
                        dma("pool", dst, src, writes=[dst], dr=[("V", s) for s in range(NTA // 4)])
                    return
                for r in range(d):
                    src = Vd[:, hd * 128:(hd + 1) * 128].rearrange("(n i dd) c -> i n dd c", i=128, dd=d)[:, nq0 - 1:nblk, r, :]
                    i0 = vblocks[(gi, r, nq0 - 1)]
                    dst = Vh[:, i0:i0 + (nblk - nq0 + 1), :]
                    dma("pool", dst, src, writes=[dst], dr=[("V", s) for s in range(NTA // 4)])

            pcnt = [0]

            pre_p = {}

            def issue_sb_loads(hd, s):
                if hd >= NH or (hd, s) in pre_p:
                    return
                par = pcnt[0] % 2
                pcnt[0] += 1
                pre_p[(hd, s)] = (load_hsb(hT, TB0 + 4 * s, par, "hT"), load_cs(LA - LB + s * 512, par))

            def proj_groups(hd):
                w = wh[hd % 2]
                qT = qTs[hd % 2]
                sgB = sgBs[hd % 3]
                groups = []
                for s in range(4):
                    for gq in range(4):
                        def emit(s=s, gq=gq):
                            if gq == 0:
                                issue_sb_loads(hd, s)
                            if gq == 2:
                                if s + 1 < 4:
                                    issue_sb_loads(hd, s + 1)
                                else:
                                    issue_sb_loads(hd + 1, 0)
                            h, (cc, ss) = pre_p[(hd, s)]
                            b = next_bank(0, 2)
                            ps = psb(b)
                            for kc in range(KC):
                                P.add("pe", lambda e, ps=ps, kc=kc, gq=gq, h=h, w=w: e.matmul(
                                    ps, lhsT=w[:, kc, gq, :], rhs=h[:, :, kc, :],
                                    start=(kc == 0), stop=(kc == KC - 1)),
                                    reads=[w[:, kc, gq, :], h[:, :, kc, :]], writes=[ps])
                            if gq < 3:
                                rope(ps, cc, ss, qT[:, gq, s * 512:(s + 1) * 512], gq % 2)
                            else:
                                P.add("act", lambda e, ps=ps, s=s, sgB=sgB: e.activation(
                                    out=sgB[:, s * 512:(s + 1) * 512], in_=ps, func=AF.Silu),
                                    reads=[ps], writes=[sgB[:, s * 512:(s + 1) * 512]])
                        groups.append(emit)
                return groups

            units = []
            for gi, d in enumerate(DILS):
                nq0 = LB // (128 * d)
                nblk = LA // (128 * d)
                blocks = [(r, n) for r in range(d) for n in range(nq0, nblk)]
                for i in range(0, len(blocks), 2):
                    units.append((gi, d, blocks[i], blocks[i + 1]))
            NU = len(units)
            assert NU == 24
            ucnt = [0]

            def kslice(start, d):
                return slice(start, start + 127 * d + 1, d) if d > 1 else slice(start, start + 128)

            def S(hd, j, slot):
                gi, d, bA, bB = units[j]
                qT = qTs[hd % 2]
                pS = pSs[slot]
                for bi, (r, n) in enumerate((bA, bB)):
                    q0 = 128 * n * d + r - (LA - LB)
                    qb = qT[:, gi, kslice(q0, d)]
                    kp = KT[:, kslice(128 * (n - 1) * d + r, d)]
                    ks = KT[:, kslice(128 * n * d + r, d)]
                    oP = pS[:, bi * 128:(bi + 1) * 128]
                    oS = pS[:, 256 + bi * 128:256 + (bi + 1) * 128]
                    P.add("pe", lambda e, oP=oP, kp=kp, qb=qb: e.matmul(oP, lhsT=kp, rhs=qb, start=True, stop=True),
                          reads=[kp, qb], writes=[oP])
                    P.add("pe", lambda e, oS=oS, ks=ks, qb=qb: e.matmul(oS, lhsT=ks, rhs=qb, start=True, stop=True),
                          reads=[ks, qb], writes=[oS])

            def pre(hd, j, slot):
                gi, d, bA, bB = units[j]
                nq0 = LB // (128 * d)
                pS = pSs[slot]
                et = ET[slot]
                em = EM[slot]
                P.add("act", lambda e, pS=pS, et=et: e.activation(out=et, in_=pS, func=AF.Exp, scale=SCALE),
                      reads=[pS], writes=[et])
                fA, fB = (bA[1] == nq0), (bB[1] == nq0)
                mi = 2 if (fA and fB) else (1 if fA else 0)
                assert not (fB and not fA)
                mk = c_masks[:, mi, :]
                P.add("dve", lambda e, et=et, em=em, mk=mk: e.tensor_tensor(out=em, in0=et, in1=mk, op=ALU.mult),
                      reads=[et, mk], writes=[em])

            def post(hd, j, slot):
                gi, d, bA, bB = units[j]
                pO = pOs[slot]
                em = EM[slot]
                acc = acc_full[hd % 2]
                anum, aden = acc_parts[hd % 2]
                for bi, (r, n) in enumerate((bA, bB)):
                    vp = Vh[:, vblocks[(gi, r, n - 1)], :]
                    vs = Vh[:, vblocks[(gi, r, n)], :]
                    o = pO[:, bi * 128:(bi + 1) * 128]
                    eP = em[:, bi * 128:(bi + 1) * 128]
                    eS = em[:, 256 + bi * 128:256 + (bi + 1) * 128]
                    P.add("pe", lambda e, o=o, vp=vp, eP=eP: e.matmul(o, lhsT=vp, rhs=eP, start=True, stop=False),
                          reads=[vp, eP], writes=[o])
                    P.add("pe", lambda e, o=o, vs=vs, eS=eS: e.matmul(o, lhsT=vs, rhs=eS, start=False, stop=True),
                          reads=[vs, eS], writes=[o])
                od = pO[:, 256:512]
                P.add("pe", lambda e, od=od, em=em: e.matmul(od, lhsT=c_ones, rhs=em[:, 0:256], start=True, stop=False),
                      reads=[c_ones, em[:, 0:256]], writes=[od])
                P.add("pe", lambda e, od=od, em=em: e.matmul(od, lhsT=c_ones, rhs=em[:, 256:512], start=False, stop=True),
                      reads=[c_ones, em[:, 256:512]], writes=[od])
                pv = pO.rearrange("p (a b k) -> p a b k", a=2, b=2)
                (rA, nA), (rB, nB) = bA, bB
                qA = 128 * nA * d + rA - (LA - LB)
                qB = 128 * nB * d + rB - (LA - LB)
                step = qB - qA
                hs = slice(qA, qA + step + 127 * d + 1)
                accv = [anum[:, hs], aden[:, hs]]
                if d == 1:
                    av = acc[:, :, qA:qA + 256].rearrange("p a (b k) -> p a b k", b=2)
                elif step == 128 * d:
                    av = acc[:, :, qA:qA + 255 * d + 1:d].rearrange("p a (b k) -> p a b k", b=2)
                else:
                    assert step == 1 and d == 16
                    assert qA == rA and nA == 1 and nB == 1
                    av = acc.rearrange("p a (k dd) -> p a dd k", dd=d)[:, :, rA:rA + 2, :]
                if gi == 0:
                    P.add("act", lambda e, av=av, pv=pv: e.activation(out=av, in_=pv, func=AF.Copy),
                          reads=[pO], writes=accv)
                else:
                    P.add("dve", lambda e, av=av, pv=pv: e.tensor_tensor(out=av, in0=pv, in1=av, op=ALU.add),
                          reads=[pO] + accv, writes=accv)

            def merge_steps(hd):
                sgB = sgBs[hd % 3]
                anum, aden = acc_parts[hd % 2]
                mhf = mh.rearrange("p t k -> p (t k)")

                def s_recip():
                    P.add("act", lambda e: e.activation(out=aden, in_=aden, func=AF.Ln), reads=[aden], writes=[aden])
                    P.add("act", lambda e: e.activation(out=aden, in_=aden, func=AF.Exp, scale=-1.0),
                          reads=[aden], writes=[aden])

                def s_mul1():
                    P.add("dve", lambda e: e.tensor_tensor(out=anum, in0=anum, in1=aden, op=ALU.mult),
                          reads=[anum, aden], writes=[anum])

                def s_mul2():
                    P.add("dve", lambda e: e.tensor_tensor(out=mhf, in0=anum, in1=sgB, op=ALU.mult),
                          reads=[anum, sgB], writes=[mhf])

                def s_store():
                    dst = mT[TB0:NTA, :, hd, :].rearrange("t p k -> p t k")
                    dma("sp", dst, mh, reads=[mh], dw=[("mTh", TB0 + i, hd) for i in range(NTB)])
                return [s_recip, s_mul1, s_mul2, s_store]

            load_wh(0)
            load_wh(1)
            for g in proj_groups(0):
                g()
            load_kt(0)
            for gi in range(3):
                load_v(0, gi)
            pending = []
            for hd in range(NH):
                if hd + 2 < NH:
                    load_wh(hd + 2)
                pg = proj_groups(hd + 1) if hd + 1 < NH else []
                gp = 0
                base = ucnt[0]
                ucnt[0] += NU

                def sl(j):
                    return (base + j) % 3
                S(hd, 0, sl(0))
                S(hd, 1, sl(1))
                pre(hd, 0, sl(0))
                for j in range(NU):
                    if j + 2 < NU:
                        S(hd, j + 2, sl(j + 2))
                        if j + 2 == NU - 1 and hd + 1 < NH:
                            load_kt(hd + 1)
                    if j + 1 < NU:
                        pre(hd, j + 1, sl(j + 1))
                    k = ((j + 1) * len(pg)) // NU - (j * len(pg)) // NU
                    for _ in range(k):
                        pg[gp]()
                        gp += 1
                    post(hd, j, sl(j))
                    if hd + 1 < NH and j in (7, 15, 23):
                        load_v(hd + 1, j // 8)
                    if pending and j in (1, 3, 5, 9):
                        pending.pop(0)()
                assert gp == len(pg)
                assert not pending
                pending = merge_steps(hd)
            for st in pending:
                st()

        stage_norm0()
        slot = [0]

        def wslot():
            b = WA if slot[0] % 2 == 0 else WB
            slot[0] += 1
            return b

        allA = list(range(NTA))
        allB = list(range(TB0, NTA))
        for g in range(4):
            stage_ag(0, g, wslot(), first_loaded=(g > 0), prefetch_next=(g < 3))
        stage_ao(w_out_a[0], wslot(), x_in, xsA, allA, [(1, hT, "hT", 0)], xkeysrc=None, xkeydst="xsA")
        for g in range(4):
            stage_ag(1, g, wslot(), first_loaded=(g > 0), prefetch_next=(g < 3))
        stage_ao(w_out_a[1], wslot(), xsA, xsB, allA, [(2, hTkv, "hTkv", 0), (3, hT, "hT", TB0)],
                 xkeysrc="xsA", xkeydst="xsB")
        stage_k(WA)
        stage_v(WB, WA + 48 * KB)
        stage_bh(0)
        stage_ao(w_out_b[0], WB, xsB, xsA, allB, [(4, hT, "hT", TB0)], xkeysrc="xsB", xkeydst="xsA")
        stage_bh(1)
        stage_ao(w_out_b[1], WB, xsA, None, allB, [], final=True, xkeysrc="xsA")

        engs = {"pe": nc.tensor, "act": nc.scalar, "dve": nc.vector, "pool": nc.gpsimd, "sp": nc.sync}
        import contextlib
        with contextlib.ExitStack() as es:
            sems = {k: es.enter_context(nc.semaphore("s_" + k)) for k in ("pe", "act", "dve", "pool")}
            dma_sems = {q: [es.enter_context(nc.semaphore("d_%s_%d" % (q, i))) for i in range(NDMASEM)]
                        for q in ("sp", "pool")}
            P.analyze(sems, dma_sems)
            block = es.enter_context(nc.Block())

            @block.tensor
            def _(e):
                P.emit("pe", e)

            @block.scalar
            def _(e):
                P.emit("act", e)

            @block.vector
            def _(e):
                P.emit("dve", e)

            @block.gpsimd
            def _(e):
                P.emit("pool", e)

            @block.sync
            def _(e):
                P.emit("sp", e)
                for s, c in P.final_dma.values():
                    e.wait_ge(s, c)
    return nc, len(P.ops)


_CACHE = {}


def _rope_tables(pos):
    inv_freq = (1.0 / (10000.0 ** (np.arange(0, HD, 2, dtype=np.float32) / np.float32(HD)))).astype(np.float32)
    ang = pos.astype(np.float32)[:, None] * inv_freq[None, :]
    c = np.cos(ang).astype(np.float32)
    s = np.sin(ang).astype(np.float32)
    cosT = np.concatenate([c.T, c.T], axis=0)
    sinT = np.concatenate([-s.T, s.T], axis=0)
    return np.ascontiguousarray(cosT), np.ascontiguousarray(sinT)


def _fm(vec):
    return np.ascontiguousarray(vec.reshape(16, 128).T)


def make_in_maps(x, norm_a, w_in_a, w_grp_a, scale_a, w_out_a, norm_kv, w_k, w_v, norm_b, w_in_b, w_out_b, norm_f):
    f32 = np.float32
    gains = np.concatenate([_fm(norm_a[0]), _fm(norm_a[1]), _fm(norm_kv), _fm(norm_b[0]), _fm(norm_b[1])], axis=1).astype(f32)
    scale_fm = np.concatenate([_fm(scale_a[0]), _fm(scale_a[1])], axis=1).astype(f32)
    kk = np.arange(128)[:, None]
    qq = np.arange(128)[None, :]
    m_prev = (kk >= qq).astype(f32)
    m_same = (kk <= qq).astype(f32)
    maps = []
    for c in range(8):
        b, h = c // 2, c % 2
        if h == 1:
            xin = np.ascontiguousarray(x[b])
            pos = np.arange(LA)
        else:
            xin = np.concatenate([np.zeros((LA - LB, D), f32), x[b, :LB]], axis=0)
            pos = np.arange(LA) - (LA - LB)
        cosT, sinT = _rope_tables(np.maximum(pos, 0))
        invcnt = np.zeros((128, 2, 4, 16), f32)
        for g, w in enumerate(POOLW):
            real = 1.0 / np.minimum(np.arange(16) + 1, w).astype(f32)
            plain = np.full(16, 1.0 / w, f32)
            invcnt[:, 0, g, :] = real if h == 1 else plain
            invcnt[:, 1, g, :] = plain if h == 1 else real
        flag = 1.0 if h == 1 else 0.0
        masks = np.zeros((128, 3, 512), f32)
        for mi, (fa, fb) in enumerate(((1.0, 1.0), (flag, 1.0), (flag, flag))):
            masks[:, mi, 0:128] = m_prev * fa
            masks[:, mi, 128:256] = m_prev * fb
            masks[:, mi, 256:384] = m_same
            masks[:, mi, 384:512] = m_same
        maps.append({
            "x_in": xin.astype(f32), "w_in_a": w_in_a, "w_grp_a": w_grp_a, "w_out_a": w_out_a,
            "w_k": w_k, "w_v": w_v, "w_in_b": w_in_b, "w_out_b": w_out_b,
            "gains": gains, "gf": np.ascontiguousarray(norm_f.reshape(1, D)).astype(f32),
            "scale_fm": scale_fm, "invcnt": invcnt.reshape(128, -1),
            "cosT": cosT, "sinT": sinT,
            "masks": masks.reshape(128, -1).astype(ml_dtypes.bfloat16),
        })
    return maps


def kernel(x, norm_a, w_in_a, w_grp_a, scale_a, w_out_a, norm_kv, w_k, w_v, norm_b, w_in_b, w_out_b, norm_f):
    args = [np.asarray(a, dtype=np.float32) for a in
            (x, norm_a, w_in_a, w_grp_a, scale_a, w_out_a, norm_kv, w_k, w_v, norm_b, w_in_b, w_out_b, norm_f)]
    if "nc" not in _CACHE:
        _CACHE["nc"] = build_program()[0]
    nc = _CACHE["nc"]
    maps = make_in_maps(*args)
    res = run_bass_kernel_spmd(nc, maps, core_ids=list(range(8)))
    out = np.zeros((4, 4096, D), np.float32)
    for c in range(8):
        b, h = c // 2, c % 2
        out[b, h * LB:(h + 1) * LB] = res.results[c]["out"]
    return out
```

```python
import math
import numpy as np
import ml_dtypes
import concourse.bass as bass
import concourse.mybir as mybir
from concourse.bass_utils import run_bass_kernel_spmd

F32 = mybir.dt.float32
BF16 = mybir.dt.bfloat16
AF = mybir.ActivationFunctionType
ALU = mybir.AluOpType

D = 2048
KC = 16
LA = 4096
LB = 2048
NTA = LA // 128
NTB = LB // 128
TB0 = NTA - NTB
EPS = 1e-6
POOLW = (2, 4, 8, 16)
DILS = (1, 4, 16)
HD = 128
NH = 16
SCALE = 1.0 / math.sqrt(HD)
PAGE = 256
NDMASEM = 12

KB = 1024
WA = 0
WB = 64 * KB
HS0 = 128 * KB
HS1 = 144 * KB
XR = 160 * KB
CR = 184 * KB
ARENA = 196 * KB


def _esz(dt):
    return 4 if dt == F32 else 2


class Prog:
    def __init__(self, nc):
        self.nc = nc
        self.ops = []

    def add(self, eng, fn, reads=(), writes=(), dr=(), dw=(), dma=False):
        self.ops.append(dict(eng=eng, fn=fn, reads=list(reads), writes=list(writes), dr=list(dr),
                             dw=list(dw), dma=dma, signal=False, waits=None))

    @staticmethod
    def keys(ap):
        dims = ap.ap
        pstride, pcount = dims[0]
        p0 = ap.base_partition()
        esz = _esz(ap.dtype)
        col = ap.offset - p0 * pstride
        name = ap.tensor.name
        page = 2048 if name == "PS" else PAGE
        big = [(s, n) for s, n in dims[1:] if n > 1 and s * esz >= page]
        small = [(s, n) for s, n in dims[1:] if not (n > 1 and s * esz >= page)]
        offsets = [0]
        for s, n in big:
            if len(offsets) * n > 512:
                small.append((s, n))
                continue
            offsets = [o + i * s for o in offsets for i in range(n)]
        span = sum((n - 1) * s for s, n in small) + 1
        pages = set()
        for o in offsets:
            lo_b = (col + o) * esz
            hi_b = (col + o + span) * esz
            pages.update(range(lo_b // page, (hi_b - 1) // page + 1))
        q0, q1 = p0 // 32, (p0 + pcount - 1) // 32
        return [(name, pg, q) for pg in pages for q in range(q0, q1 + 1)]

    def analyze(self, sems, dma_sems):
        lastw = {}
        readers = {}
        eng_sig = {}
        dma_cnt = {}
        ops = self.ops
        for i, op in enumerate(ops):
            deps = set()
            rk = []
            wk = []
            for ap in op["reads"]:
                rk.extend(self.keys(ap))
            for k in op["dr"]:
                rk.append(("dram", k))
            for ap in op["writes"]:
                wk.extend(self.keys(ap))
            for k in op["dw"]:
                wk.append(("dram", k))
            for k in rk:
                w = lastw.get(k)
                if w is not None:
                    deps.add(w)
                if k[0] == "PS":
                    r = readers.get(k)
                    if r:
                        for en, j in r.items():
                            if en != op["eng"]:
                                deps.add(j)
            for k in wk:
                w = lastw.get(k)
                if w is not None:
                    deps.add(w)
                r = readers.get(k)
                if r:
                    deps.update(r.values())
            for k in rk:
                readers.setdefault(k, {})[op["eng"]] = i
            for k in wk:
                lastw[k] = i
                readers[k] = {}
            deps.discard(i)
            op["deps"] = deps
            for j in deps:
                oj = ops[j]
                if oj["eng"] == "pe" and op["eng"] == "pe" and not oj["dma"] and not op["dma"]:
                    continue
                oj["signal"] = True
        for i, op in enumerate(ops):
            e = op["eng"]
            if op["dma"]:
                k = dma_cnt.get(e, 0)
                dma_cnt[e] = k + 1
                nsem = 3 if e == "pool" else NDMASEM
                sem = dma_sems[e][k % nsem]
                op["sem"] = sem
                op["cnt"] = 16 * (k // nsem + 1)
                op["prev"] = 16 * (k // nsem)
                op["signal"] = True
            elif op["signal"]:
                c = eng_sig.get(e, 0) + 1
                eng_sig[e] = c
                op["sem"] = sems[e]
                op["cnt"] = c
        waited = {}
        for i, op in enumerate(ops):
            e = op["eng"]
            need = {}
            for j in op["deps"]:
                oj = ops[j]
                if oj["eng"] == "pe" and e == "pe" and not oj["dma"] and not op["dma"]:
                    continue
                s = oj["sem"]
                key = id(s)
                if key not in need or need[key][1] < oj["cnt"]:
                    need[key] = (s, oj["cnt"])
            if op["dma"] and op["prev"] > 0:
                s = op["sem"]
                key = id(s)
                if key not in need or need[key][1] < op["prev"]:
                    need[key] = (s, op["prev"])
            w = []
            wd = waited.setdefault(e, {})
            for key, (s, c) in need.items():
                if wd.get(key, 0) >= c:
                    continue
                wd[key] = c
                w.append((s, c))
            op["waits"] = w
        self.final_dma = {}
        for op in ops:
            if op["dma"]:
                self.final_dma[id(op["sem"])] = (op["sem"], op["cnt"])

    def emit(self, engname, eng):
        for op in self.ops:
            if op["eng"] != engname:
                continue
            waits = op["waits"]
            if engname == "pe" and waits:
                for s, c in waits[1:]:
                    eng.wait_ge(s, c)
                ins = op["fn"](eng)
                ins._wait_ge(waits[0][0], waits[0][1])
            else:
                for s, c in waits:
                    eng.wait_ge(s, c)
                ins = op["fn"](eng)
            if op["signal"]:
                ins.then_inc(op["sem"], 16 if op["dma"] else 1)


def build_program(debug=False):
    nc = bass.Bass("TRN2", target_bir_lowering=False)
    dt = nc.dram_tensor
    x_in = dt("x_in", [LA, D], F32, kind="ExternalInput").ap()
    w_in_a = dt("w_in_a", [2, D, 2 * D], F32, kind="ExternalInput").ap()
    w_grp_a = dt("w_grp_a", [2, 4, 512, 512], F32, kind="ExternalInput").ap()
    w_out_a = dt("w_out_a", [2, D, D], F32, kind="ExternalInput").ap()
    w_k = dt("w_k", [D, D], F32, kind="ExternalInput").ap()
    w_v = dt("w_v", [D, D], F32, kind="ExternalInput").ap()
    w_in_b = dt("w_in_b", [2, D, 4 * D], F32, kind="ExternalInput").ap()
    w_out_b = dt("w_out_b", [2, D, D], F32, kind="ExternalInput").ap()
    gains_d = dt("gains", [128, 5 * 16], F32, kind="ExternalInput").ap()
    gf_d = dt("gf", [1, D], F32, kind="ExternalInput").ap()
    scale_d = dt("scale_fm", [128, 2 * 16], F32, kind="ExternalInput").ap()
    invcnt_d = dt("invcnt", [128, 2 * 4 * 16], F32, kind="ExternalInput").ap()
    cos_d = dt("cosT", [128, LA], F32, kind="ExternalInput").ap()
    sin_d = dt("sinT", [128, LA], F32, kind="ExternalInput").ap()
    masks_d = dt("masks", [128, 3 * 512], BF16, kind="ExternalInput").ap()
    out_d = dt("out", [LB, D], F32, kind="ExternalOutput").ap()
    skind = "ExternalOutput" if debug else "Internal"
    xsA = dt("xsA", [LA, D], F32, kind=skind).ap()
    xsB = dt("xsB", [LA, D], F32, kind=skind).ap()
    hT = dt("hT", [NTA, 128, KC, 128], BF16, kind=skind).ap()
    hTkv = dt("hTkv", [NTA, 128, KC, 128], BF16, kind=skind).ap()
    mT = dt("mT", [NTA, 128, KC, 128], BF16, kind=skind).ap()
    KTd = dt("KTd", [NH, 128, LA], BF16, kind=skind).ap()
    Vd = dt("Vd", [LA, D], BF16, kind=skind).ap()

    P = Prog(nc)

    with (nc.sbuf_tensor("SB", [128, ARENA // 4], F32) as SBT,
          nc.psum_tensor("PS", [128, 4096], F32) as PST):

        def sb(off, shape, dtp=BF16):
            esz = _esz(dtp)
            n = int(np.prod(shape))
            assert off % 4 == 0 and (n * esz) % 4 == 0
            ap = SBT[:, off // 4:(off + n * esz) // 4]
            if dtp != F32:
                ap = ap.bitcast(dtp)
            if len(shape) == 2:
                ap = ap.rearrange("p (a b) -> p a b", b=shape[1])
            elif len(shape) == 3:
                ap = ap.rearrange("p (a b c) -> p a b c", b=shape[1], c=shape[2])
            return ap

        def psb(bank, nbanks=1, dtp=F32):
            ap = PST[:, bank * 512:(bank + nbanks) * 512]
            if dtp != F32:
                ap = ap.bitcast(dtp)
            return ap

        c_ident = sb(CR + 0, [128], BF16)
        c_ones = sb(CR + 256, [128], BF16)
        c_gains = sb(CR + 512, [5, 16], F32)
        c_scale = sb(CR + 832, [2, 16], F32)
        c_invcnt = sb(CR + 960, [2, 4, 16], F32)
        c_eps = sb(CR + 1472, [1], F32)
        c_identf = sb(CR + 1536, [128], F32)
        c_stats = sb(CR + 2048, [4, 4], F32)
        c_masks = sb(CR + 2560, [3, 512], BF16)

        def dma(eng, out, in_, reads=(), writes=(), dr=(), dw=()):
            P.add(eng, lambda e, o=out, i=in_: e.dma_start(out=o, in_=i), reads=reads, writes=writes,
                  dr=dr, dw=dw, dma=True)

        dma("sp", c_masks, masks_d.rearrange("p (a b) -> p a b", b=512), writes=[c_masks])
        dma("sp", c_gains, gains_d.rearrange("p (a b) -> p a b", b=16), writes=[c_gains])
        dma("sp", c_scale, scale_d.rearrange("p (a b) -> p a b", b=16), writes=[c_scale])
        dma("sp", c_invcnt, invcnt_d.rearrange("p (a b c) -> p a b c", b=4, c=16), writes=[c_invcnt])
        P.add("pool", lambda e: e.memset(c_identf, 0.0), writes=[c_identf])
        P.add("pool", lambda e: e.affine_select(out=c_identf, in_=c_identf, pattern=[[-1, 128]],
                                                compare_op=ALU.not_equal, fill=1.0, base=0,
                                                channel_multiplier=1),
              reads=[c_identf], writes=[c_identf])
        P.add("dve", lambda e: e.tensor_copy(out=c_ident, in_=c_identf), reads=[c_identf], writes=[c_ident])
        P.add("dve", lambda e: e.memset(c_ones, 1.0), writes=[c_ones])
        P.add("dve", lambda e: e.memset(c_eps, EPS), writes=[c_eps])

        bank_rr = [0]

        def next_bank(lo=0, hi=8):
            b = bank_rr[0]
            if b < lo or b >= hi:
                b = lo
            bank_rr[0] = b + 1 if b + 1 < hi else lo
            return b

        def load_w(dst, src):
            n = dst.shape[2]
            s3 = src.rearrange("(kc p) n -> p kc n", p=128)
            step = 512
            for c0 in range(0, n, step):
                c1 = min(n, c0 + step)
                dma("pool", dst[:, :, c0:c1], s3[:, :, c0:c1], writes=[dst[:, :, c0:c1]])

        def normA(xt, xn, si, need_xn=True):
            ssq = c_stats[:, si, 0:1]
            std = c_stats[:, si, 1:2]
            rstd = c_stats[:, si, 2:3]
            P.add("act", lambda e: e.activation(out=xn, in_=xt, func=AF.Square, accum_out=ssq),
                  reads=[xt], writes=[xn, ssq])
            P.add("act", lambda e: e.activation(out=std, in_=ssq, func=AF.Sqrt, bias=c_eps, scale=1.0 / D),
                  reads=[ssq, c_eps], writes=[std])
            P.add("dve", lambda e: e.reciprocal(out=rstd, in_=std), reads=[std], writes=[rstd])
            if need_xn:
                P.add("act", lambda e: e.activation(out=xn, in_=xt, func=AF.Copy, scale=rstd),
                      reads=[xt, rstd], writes=[xn])
            return rstd

        def normB(xn, pst, outs):
            for c in range(KC):
                P.add("pe", lambda e, c=c: e.transpose(out=pst[:, c, :], in_=xn[:, c * 128:(c + 1) * 128],
                                                       identity=c_ident),
                      reads=[xn[:, c * 128:(c + 1) * 128], c_ident], writes=[pst[:, c, :]])
            for (gi, hto, dst, key) in outs:
                g = c_gains[:, gi, :].unsqueeze(2).to_broadcast([128, KC, 128])
                P.add("dve", lambda e, hto=hto, g=g: e.tensor_tensor(out=hto, in0=pst, in1=g, op=ALU.mult),
                      reads=[pst, c_gains], writes=[hto])
                dma("sp", dst, hto, reads=[hto], dw=[key])

        def pst_view(bank):
            return psb(bank, 2, BF16).rearrange("p (c k) -> p c k", k=128)

        def hto_buf(i):
            return sb(HS0 + 8 * KB + i * 4 * KB, [KC, 128], BF16)

        def stage_norm0():
            xts = [sb(WB + i * 8 * KB, [D], F32) for i in range(4)]
            xns = [sb(WB + 32 * KB + i * 4 * KB, [D], BF16) for i in range(4)]
            htos = [sb(WB + 48 * KB + i * 4 * KB, [KC, 128], BF16) for i in range(4)]
            psts = [pst_view(4), pst_view(6)]

            def load(T):
                dma("sp", xts[T % 4], x_in[T * 128:(T + 1) * 128, :], writes=[xts[T % 4]])

            def partB(T):
                normB(xns[T % 4], psts[T % 2], [(0, htos[T % 4], hT[T], ("hT", T))])

            load(0)
            load(1)
            for T in range(NTA):
                if T + 2 < NTA:
                    load(T + 2)
                normA(xts[T % 4], xns[T % 4], T % 4)
                if T >= 1:
                    partB(T - 1)
            partB(NTA - 1)

        def stage_ag(l, g, wbase, first_loaded=False, prefetch_next=False):
            Wu = sb(wbase, [KC, 512], BF16)
            Wg = sb(wbase + 16 * KB, [KC, 512], BF16)
            Wgrp = sb(wbase + 32 * KB, [4, 512], BF16)
            load_w(Wu, w_in_a[l][:, g * 512:(g + 1) * 512])
            load_w(Wg, w_in_a[l][:, D + g * 512:D + (g + 1) * 512])
            load_w(Wgrp, w_grp_a[l][g])
            U = [sb(WA + 36 * KB + i * 8448, [4, 528], F32) for i in range(2)]
            T1 = sb(WA + 36 * KB + 2 * 8448, [4, 528], F32)
            T2 = sb(WB + 36 * KB, [4, 528], F32)
            sg = sb(WB + 36 * KB + 8448, [4, 512], BF16)
            pooled = sb(WB + 36 * KB + 8448 + 4096, [4, 512], BF16)
            mbuf = [sb(XR + i * 4 * KB, [4, 4, 128], BF16) for i in range(2)]
            hsb = [sb(HS0, [4, KC, 128], BF16), sb(HS1, [4, KC, 128], BF16)]
            w = POOLW[g]
            nsb = NTA // 4

            def load(s):
                h = hsb[s % 2]
                src = hT[4 * s:4 * s + 4].rearrange("t p c k -> p t (c k)")
                dma("sp", h.rearrange("p t c k -> p t (c k)"), src, writes=[h],
                    dr=[("hT", 4 * s + i) for i in range(4)])

            def ug(s):
                h = hsb[s % 2]
                u = U[s % 2]
                for j in range(4):
                    b = next_bank(0, 6)
                    ps = psb(b)
                    for kc in range(KC):
                        P.add("pe", lambda e, ps=ps, kc=kc, j=j, h=h: e.matmul(
                            ps, lhsT=Wu[:, kc, j * 128:(j + 1) * 128], rhs=h[:, :, kc, :],
                            start=(kc == 0), stop=(kc == KC - 1)),
                            reads=[Wu[:, kc, j * 128:(j + 1) * 128], h[:, :, kc, :]], writes=[ps])
                    P.add("act", lambda e, ps=ps, u=u, j=j: e.activation(out=u[:, j, 16:528], in_=ps, func=AF.Copy),
                          reads=[ps], writes=[u[:, j, 16:528]])
                if s == 0:
                    P.add("dve", lambda e, u=u: e.memset(u[:, :, 0:16], 0.0), writes=[u[:, :, 0:16]])
                else:
                    up = U[(s - 1) % 2]
                    P.add("dve", lambda e, u=u, up=up: e.tensor_copy(out=u[:, :, 0:16], in_=up[:, :, 512:528]),
                          reads=[up[:, :, 512:528]], writes=[u[:, :, 0:16]])

            def gate(s):
                h = hsb[s % 2]
                for j in range(4):
                    b = next_bank(0, 6)
                    ps = psb(b)
                    for kc in range(KC):
                        P.add("pe", lambda e, ps=ps, kc=kc, j=j, h=h: e.matmul(
                            ps, lhsT=Wg[:, kc, j * 128:(j + 1) * 128], rhs=h[:, :, kc, :],
                            start=(kc == 0), stop=(kc == KC - 1)),
                            reads=[Wg[:, kc, j * 128:(j + 1) * 128], h[:, :, kc, :]], writes=[ps])
                    P.add("act", lambda e, ps=ps, j=j: e.activation(out=sg[:, j, :], in_=ps, func=AF.Silu),
                          reads=[ps], writes=[sg[:, j, :]])

            def pool(s):
                u = U[s % 2]
                src = u
                bufs = [T1, T2]
                sh = 1
                lvl = 0
                while sh < w:
                    dst = bufs[lvl % 2]
                    lo = 2 * sh - 1
                    P.add("dve", lambda e, dst=dst, src=src, lo=lo, sh=sh: e.tensor_tensor(
                        out=dst[:, :, lo:528], in0=src[:, :, lo:528], in1=src[:, :, lo - sh:528 - sh], op=ALU.add),
                        reads=[src], writes=[dst[:, :, lo:528]])
                    src = dst
                    sh *= 2
                    lvl += 1
                P.add("dve", lambda e, src=src, u=u: e.scalar_tensor_tensor(
                    out=pooled, in0=src[:, :, 16:528], scalar=1.0 / w, in1=u[:, :, 16:528],
                    op0=ALU.mult, op1=ALU.subtract),
                    reads=[src, u], writes=[pooled])
                if s in (0, nsb // 2):
                    loc = 0 if s == 0 else 1
                    ic = c_invcnt[:, loc, g, :].unsqueeze(1).to_broadcast([128, 4, 16])
                    tmp = T1 if src is T2 else T2
                    P.add("dve", lambda e, src=src, ic=ic, tmp=tmp: e.tensor_tensor(
                        out=tmp[:, :, 0:16], in0=src[:, :, 16:32], in1=ic, op=ALU.mult),
                        reads=[src, c_invcnt], writes=[tmp[:, :, 0:16]])
                    P.add("dve", lambda e, tmp=tmp, u=u: e.tensor_tensor(
                        out=pooled[:, :, 0:16], in0=tmp[:, :, 0:16], in1=u[:, :, 16:32], op=ALU.subtract),
                        reads=[tmp[:, :, 0:16], u], writes=[pooled[:, :, 0:16]])

            def ymm(s):
                mb = mbuf[s % 2]
                for jo in range(4):
                    b = next_bank(0, 6)
                    ps = psb(b)
                    for ji in range(4):
                        P.add("pe", lambda e, ps=ps, ji=ji, jo=jo: e.matmul(
                            ps, lhsT=Wgrp[:, ji, jo * 128:(jo + 1) * 128], rhs=pooled[:, ji, :],
                            start=(ji == 0), stop=(ji == 3)),
                            reads=[Wgrp[:, ji, jo * 128:(jo + 1) * 128], pooled[:, ji, :]], writes=[ps])
                    sc = c_scale[:, l, g * 4 + jo:g * 4 + jo + 1]
                    P.add("dve", lambda e, ps=ps, mb=mb, jo=jo, sc=sc: e.scalar_tensor_tensor(
                        out=mb[:, :, jo, :], in0=ps.rearrange("p (t k) -> p t k", k=128), scalar=sc,
                        in1=sg[:, jo, :].rearrange("p (t k) -> p t k", k=128), op0=ALU.mult, op1=ALU.mult),
                        reads=[ps, sc, sg[:, jo, :]], writes=[mb[:, :, jo, :]])
                dst = mT[4 * s:4 * s + 4, :, 4 * g:4 * g + 4, :].rearrange("t p c k -> p t (c k)")
                dma("sp", dst, mb.rearrange("p t c k -> p t (c k)"), reads=[mb],
                    dw=[("mT", 4 * s + i, g) for i in range(4)])

            if not first_loaded:
                load(0)
            ug(0)
            gate(0)
            for s in range(nsb):
                if s + 1 < nsb:
                    load(s + 1)
                elif prefetch_next:
                    load(0)
                pool(s)
                if s + 1 < nsb:
                    ug(s + 1)
                ymm(s)
                if s + 1 < nsb:
                    gate(s + 1)

        def stage_ao(wsrc, wbase, xsrc, xdst, tiles, norm_outs, final=False, xkeysrc=None, xkeydst=None):
            Wo = sb(wbase, [KC, D], BF16)
            load_w(Wo, wsrc)
            if final:
                gfb = sb(HS1, [D], F32)
                dma("sp", gfb, gf_d[0:1, :].partition_broadcast(128), writes=[gfb])

            def load(T):
                par = T % 2
                xt = sb(XR + par * 8 * KB, [D], F32)
                mt = sb(HS0 + par * 4 * KB, [KC, 128], BF16)
                dma("sp", xt, xsrc[T * 128:(T + 1) * 128, :], writes=[xt],
                    dr=([(xkeysrc, T)] if xkeysrc else []))
                dma("sp", mt.rearrange("p c k -> p (c k)"), mT[T].rearrange("p c k -> p (c k)"), writes=[mt],
                    dr=[("mT", T, g) for g in range(4)] + [("mTh", T, h) for h in range(NH)])

            def comp(T):
                par = T % 2
                xt = sb(XR + par * 8 * KB, [D], F32)
                mt = sb(HS0 + par * 4 * KB, [KC, 128], BF16)
                xn = sb(XR + 16 * KB + par * 4 * KB, [D], BF16)
                for n in range(4):
                    b = next_bank(0, 6)
                    ps = psb(b)
                    for kc in range(KC):
                        P.add("pe", lambda e, ps=ps, kc=kc, n=n, mt=mt: e.matmul(
                            ps, lhsT=mt[:, kc, :], rhs=Wo[:, kc, n * 512:(n + 1) * 512],
                            start=(kc == 0), stop=(kc == KC - 1)),
                            reads=[mt[:, kc, :], Wo[:, kc, n * 512:(n + 1) * 512]], writes=[ps])
                    xs = xt[:, n * 512:(n + 1) * 512]
                    P.add("dve", lambda e, ps=ps, xs=xs: e.tensor_tensor(out=xs, in0=ps, in1=xs, op=ALU.add),
                          reads=[ps, xs], writes=[xs])
                if xdst is not None:
                    dma("sp", xdst[T * 128:(T + 1) * 128, :], xt, reads=[xt], dw=[(xkeydst, T)])
                outs = []
                for i, (gi, dstT, keyname, tmin) in enumerate(norm_outs):
                    if T >= tmin:
                        outs.append((gi, hto_buf(2 * i + par), dstT[T], (keyname, T)))
                rstd = normA(xt, xn, par, need_xn=bool(outs))
                if final:
                    ot = sb(HS1 + 8 * KB, [D], F32) if par == 0 else sb(HS0 + 8 * KB, [D], F32)
                    P.add("dve", lambda e, ot=ot, xt=xt, rstd=rstd: e.scalar_tensor_tensor(
                        out=ot, in0=xt, scalar=rstd, in1=gfb, op0=ALU.mult, op1=ALU.mult),
                        reads=[xt, rstd, gfb], writes=[ot])
                    dma("sp", out_d[(T - TB0) * 128:(T - TB0 + 1) * 128, :], ot, reads=[ot])
                return (xn, outs)

            load(tiles[0])
            prev = None
            for i, T in enumerate(tiles):
                if i + 1 < len(tiles):
                    load(tiles[i + 1])
                cur = comp(T)
                if prev is not None and prev[1]:
                    normB(prev[0], pst_view(6), prev[1])
                prev = cur
            if prev is not None and prev[1]:
                normB(prev[0], pst_view(6), prev[1])

        def rope(ps, cs_c, cs_s, dst, par):
            t1 = sb(XR + 16 * KB + par * 4 * KB, [512], F32)
            t2 = sb(XR + 16 * KB + 2 * KB + par * 4 * KB, [512], F32)
            P.add("dve", lambda e: e.tensor_tensor(out=t1, in0=ps, in1=cs_c, op=ALU.mult),
                  reads=[ps, cs_c], writes=[t1])
            P.add("dve", lambda e: e.tensor_tensor(out=t2[0:64, :], in0=ps[64:128, :], in1=cs_s[0:64, :], op=ALU.mult),
                  reads=[ps[64:128, :], cs_s[0:64, :]], writes=[t2[0:64, :]])
            P.add("dve", lambda e: e.tensor_tensor(out=t2[64:128, :], in0=ps[0:64, :], in1=cs_s[64:128, :], op=ALU.mult),
                  reads=[ps[0:64, :], cs_s[64:128, :]], writes=[t2[64:128, :]])
            P.add("dve", lambda e: e.tensor_tensor(out=dst, in0=t1, in1=t2, op=ALU.add),
                  reads=[t1, t2], writes=[dst])

        def load_cs(s_tok0, par):
            cc = sb(XR + par * 4 * KB, [512], F32)
            ss = sb(XR + 2 * KB + par * 4 * KB, [512], F32)
            dma("sp", cc, cos_d[:, s_tok0:s_tok0 + 512], writes=[cc])
            dma("sp", ss, sin_d[:, s_tok0:s_tok0 + 512], writes=[ss])
            return cc, ss

        def load_hsb(src_tiles, s4, par, keyname):
            h = sb(HS0 if par == 0 else HS1, [4, KC, 128], BF16)
            src = src_tiles[s4:s4 + 4].rearrange("t p c k -> p t (c k)")
            dma("sp", h.rearrange("p t c k -> p t (c k)"), src, writes=[h],
                dr=[(keyname, s4 + i) for i in range(4)])
            return h

        def stage_k(wbase):
            Wk = sb(wbase, [KC, D], BF16)
            load_w(Wk, w_k)
            nsb = NTA // 4
            cnt = [0]
            pre_k = {0: (load_hsb(hTkv, 0, 0, "hTkv"), load_cs(0, 0))}
            for s in range(nsb):
                h, (cc, ss) = pre_k.pop(s)
                for hd in range(NH):
                    if hd == 4 and s + 1 < nsb:
                        pre_k[s + 1] = (load_hsb(hTkv, 4 * (s + 1), (s + 1) % 2, "hTkv"),
                                        load_cs((s + 1) * 512, (s + 1) % 2))
                    b = next_bank(0, 8)
                    ps = psb(b)
                    for kc in range(KC):
                        P.add("pe", lambda e, ps=ps, kc=kc, hd=hd, h=h: e.matmul(
                            ps, lhsT=Wk[:, kc, hd * 128:(hd + 1) * 128], rhs=h[:, :, kc, :],
                            start=(kc == 0), stop=(kc == KC - 1)),
                            reads=[Wk[:, kc, hd * 128:(hd + 1) * 128], h[:, :, kc, :]], writes=[ps])
                    par = cnt[0] % 2
                    cnt[0] += 1
                    ko = sb(XR + 8 * KB + par * KB, [512], BF16)
                    rope(ps, cc, ss, ko, par)
                    dma("sp", KTd[hd][:, s * 512:(s + 1) * 512], ko, reads=[ko], dw=[("KT", hd, s)])

        def stage_v(wbase, vtalt):
            Wv = sb(wbase, [KC, D], BF16)
            load_w(Wv, w_v)
            nsb = NTA // 4
            vts = [sb(XR, [4, D], BF16), sb(vtalt, [4, D], BF16)]
            pre_v = {0: load_hsb(hTkv, 0, 0, "hTkv")}
            for s in range(nsb):
                h = pre_v.pop(s)
                vt = vts[s % 2]
                for tt in range(4):
                    if tt == 1 and s + 1 < nsb:
                        pre_v[s + 1] = load_hsb(hTkv, 4 * (s + 1), (s + 1) % 2, "hTkv")
                    for n in range(4):
                        b = next_bank(0, 8)
                        ps = psb(b)
                        for kc in range(KC):
                            P.add("pe", lambda e, ps=ps, kc=kc, n=n, tt=tt, h=h: e.matmul(
                                ps, lhsT=h[:, tt, kc, :], rhs=Wv[:, kc, n * 512:(n + 1) * 512],
                                start=(kc == 0), stop=(kc == KC - 1)),
                                reads=[h[:, tt, kc, :], Wv[:, kc, n * 512:(n + 1) * 512]], writes=[ps])
                        P.add("act", lambda e, ps=ps, vt=vt, tt=tt, n=n: e.activation(
                            out=vt[:, tt, n * 512:(n + 1) * 512], in_=ps, func=AF.Copy),
                            reads=[ps], writes=[vt[:, tt, n * 512:(n + 1) * 512]])
                dst = Vd[s * 512:(s + 1) * 512, :].rearrange("(t p) c -> p t c", p=128)
                dma("sp", dst, vt, reads=[vt], dw=[("V", s)])

        vblocks = {}
        vb_list = []
        for gi, d in enumerate(DILS):
            nq0 = LB // (128 * d)
            nblk = LA // (128 * d)
            if d == 16:
                order = [(r, n) for n in range(nq0 - 1, nblk) for r in range(d)]
            else:
                order = [(r, n) for r in range(d) for n in range(nq0 - 1, nblk)]
            for (r, n) in order:
                vblocks[(gi, r, n)] = len(vb_list)
                vb_list.append((gi, r, n))
        NVB = len(vb_list)

        def stage_bh(l):
            wh = [sb(WA + i * 16 * KB, [KC, 4, 128], BF16) for i in range(2)]
            KT = sb(WA + 32 * KB, [LA], BF16)
            qTs = [sb(WA + 40 * KB, [3, LB], BF16), sb(WB + 48 * KB, [3, LB], BF16)]
            sgBs = [sb(WA + 52 * KB, [LB], BF16), sb(WB + 60 * KB, [LB], BF16), sb(190 * KB, [LB], BF16)]
            mh = sb(WB + 18 * KB, [NTB, 128], BF16)
            ET = [sb(XR + 8 * KB + i * KB, [512], BF16) for i in range(3)]
            EM = [sb(XR + 11 * KB + i * KB, [512], BF16) for i in range(3)]
            Vh = sb(WB, [NVB, 128], BF16)
            acc_full = [sb(WB + 24 * KB, [2, LB], F32),
                        SBT[:, (WA + 56 * KB) // 4:(WA + 56 * KB) // 4 + 2 * 12288].rearrange("p (a s) -> p a s", a=2)[:, :, 0:LB]]
            acc_parts = [(sb(WB + 24 * KB, [LB], F32), sb(WB + 32 * KB, [LB], F32)),
                         (sb(WA + 56 * KB, [LB], F32), sb(WB + 40 * KB, [LB], F32))]
            pSs = [psb(2 + i) for i in range(3)]
            pOs = [psb(5 + i) for i in range(3)]

            def load_wh(hd):
                w = wh[hd % 2]
                for gq in range(4):
                    src = w_in_b[l][:, gq * D + hd * 128:gq * D + (hd + 1) * 128].rearrange("(kc p) n -> p kc n", p=128)
                    dma("pool", w[:, :, gq, :], src, writes=[w[:, :, gq, :]])

            def load_kt(hd):
                dma("sp", KT, KTd[hd], writes=[KT], dr=[("KT", hd, s) for s in range(NTA // 4)])

            def load_v(hd, gi):
                d = DILS[gi]
                nq0 = LB // (128 * d)
                nblk = LA // (128 * d)
                if d == 16:
                    for n in range(nq0 - 1, nblk):
                        src = Vd[:, hd * 128:(hd + 1) * 128].rearrange("(n i dd) c -> i n dd c", i=128, dd=d)[:, n, :, :]
                        i0 = vblocks[(gi, 0, n)]
                        dst = Vh[:, i0:i0 + d, :]
                        dma("pool", dst, src, writes=[dst], dr=[("V", s) for s in range(NTA // 4)])
                    return
                for r in range(d):
                    src = Vd[:, hd * 128:(hd + 1) * 128].rearrange("(n i dd) c -> i n dd c", i=128, dd=d)[:, nq0 - 1:nblk, r, :]
                    i0 = vblocks[(gi, r, nq0 - 1)]
                    dst = Vh[:, i0:i0 + (nblk - nq0 + 1), :]
                    dma("pool", dst, src, writes=[dst], dr=[("V", s) for s in range(NTA // 4)])

            pcnt = [0]

            pre_p = {}

            def issue_sb_loads(hd, s):
                if hd >= NH or (hd, s) in pre_p:
                    return
                par = pcnt[0] % 2
                pcnt[0] += 1
                pre_p[(hd, s)] = (load_hsb(hT, TB0 + 4 * s, par, "hT"), load_cs(LA - LB + s * 512, par))

            def proj_groups(hd):
                w = wh[hd % 2]
                qT = qTs[hd % 2]
                sgB = sgBs[hd % 3]
                groups = []
                for s in range(4):
                    for gq in range(4):
                        def emit(s=s, gq=gq):
                            if gq == 0:
                                issue_sb_loads(hd, s)
                            if gq == 2:
                                if s + 1 < 4:
                                    issue_sb_loads(hd, s + 1)
                                else:
                                    issue_sb_loads(hd + 1, 0)
                            h, (cc, ss) = pre_p[(hd, s)]
                            b = next_bank(0, 2)
                            ps = psb(b)
                            for kc in range(KC):
                                P.add("pe", lambda e, ps=ps, kc=kc, gq=gq, h=h, w=w: e.matmul(
                                    ps, lhsT=w[:, kc, gq, :], rhs=h[:, :, kc, :],
                                    start=(kc == 0), stop=(kc == KC - 1)),
                                    reads=[w[:, kc, gq, :], h[:, :, kc, :]], writes=[ps])
                            if gq < 3:
                                rope(ps, cc, ss, qT[:, gq, s * 512:(s + 1) * 512], gq % 2)
                            else:
                                P.add("act", lambda e, ps=ps, s=s, sgB=sgB: e.activation(
                                    out=sgB[:, s * 512:(s + 1) * 512], in_=ps, func=AF.Silu),
                                    reads=[ps], writes=[sgB[:, s * 512:(s + 1) * 512]])
                        groups.append(emit)
                return groups

            units = []
            for gi, d in enumerate(DILS):
                nq0 = LB // (128 * d)
                nblk = LA // (128 * d)
                blocks = [(r, n) for r in range(d) for n in range(nq0, nblk)]
                for i in range(0, len(blocks), 2):
                    units.append((gi, d, blocks[i], blocks[i + 1]))
            NU = len(units)
            assert NU == 24
            ucnt = [0]

            def kslice(start, d):
                return slice(start, start + 127 * d + 1, d) if d > 1 else slice(start, start + 128)

            def S(hd, j, slot):
                gi, d, bA, bB = units[j]
                qT = qTs[hd % 2]
                pS = pSs[slot]
                for bi, (r, n) in enumerate((bA, bB)):
                    q0 = 128 * n * d + r - (LA - LB)
                    qb = qT[:, gi, kslice(q0, d)]
                    kp = KT[:, kslice(128 * (n - 1) * d + r, d)]
                    ks = KT[:, kslice(128 * n * d + r, d)]
                    oP = pS[:, bi * 128:(bi + 1) * 128]
                    oS = pS[:, 256 + bi * 128:256 + (bi + 1) * 128]
                    P.add("pe", lambda e, oP=oP, kp=kp, qb=qb: e.matmul(oP, lhsT=kp, rhs=qb, start=True, stop=True),
                          reads=[kp, qb], writes=[oP])
                    P.add("pe", lambda e, oS=oS, ks=ks, qb=qb: e.matmul(oS, lhsT=ks, rhs=qb, start=True, stop=True),
                          reads=[ks, qb], writes=[oS])

            def pre(hd, j, slot):
                gi, d, bA, bB = units[j]
                nq0 = LB // (128 * d)
                pS = pSs[slot]
                et = ET[slot]
                em = EM[slot]
                P.add("act", lambda e, pS=pS, et=et: e.activation(out=et, in_=pS, func=AF.Exp, scale=SCALE),
                      reads=[pS], writes=[et])
                fA, fB = (bA[1] == nq0), (bB[1] == nq0)
                mi = 2 if (fA and fB) else (1 if fA else 0)
                assert not (fB and not fA)
                mk = c_masks[:, mi, :]
                P.add("dve", lambda e, et=et, em=em, mk=mk: e.tensor_tensor(out=em, in0=et, in1=mk, op=ALU.mult),
                      reads=[et, mk], writes=[em])

            def post(hd, j, slot):
                gi, d, bA, bB = units[j]
                pO = pOs[slot]
                em = EM[slot]
                acc = acc_full[hd % 2]
                anum, aden = acc_parts[hd % 2]
                for bi, (r, n) in enumerate((bA, bB)):
                    vp = Vh[:, vblocks[(gi, r, n - 1)], :]
                    vs = Vh[:, vblocks[(gi, r, n)], :]
                    o = pO[:, bi * 128:(bi + 1) * 128]
                    eP = em[:, bi * 128:(bi + 1) * 128]
                    eS = em[:, 256 + bi * 128:256 + (bi + 1) * 128]
                    P.add("pe", lambda e, o=o, vp=vp, eP=eP: e.matmul(o, lhsT=vp, rhs=eP, start=True, stop=False),
                          reads=[vp, eP], writes=[o])
                    P.add("pe", lambda e, o=o, vs=vs, eS=eS: e.matmul(o, lhsT=vs, rhs=eS, start=False, stop=True),
                          reads=[vs, eS], writes=[o])
                od = pO[:, 256:512]
                P.add("pe", lambda e, od=od, em=em: e.matmul(od, lhsT=c_ones, rhs=em[:, 0:256], start=True, stop=False),
                      reads=[c_ones, em[:, 0:256]], writes=[od])
                P.add("pe", lambda e, od=od, em=em: e.matmul(od, lhsT=c_ones, rhs=em[:, 256:512], start=False, stop=True),
                      reads=[c_ones, em[:, 256:512]], writes=[od])
                pv = pO.rearrange("p (a b k) -> p a b k", a=2, b=2)
                (rA, nA), (rB, nB) = bA, bB
                qA = 128 * nA * d + rA - (LA - LB)
                qB = 128 * nB * d + rB - (LA - LB)
                step = qB - qA
                hs = slice(qA, qA + step + 127 * d + 1)
                accv = [anum[:, hs], aden[:, hs]]
                if d == 1:
                    av = acc[:, :, qA:qA + 256].rearrange("p a (b k) -> p a b k", b=2)
                elif step == 128 * d:
                    av = acc[:, :, qA:qA + 255 * d + 1:d].rearrange("p a (b k) -> p a b k", b=2)
                else:
                    assert step == 1 and d == 16
                    assert qA == rA and nA == 1 and nB == 1
                    av = acc.rearrange("p a (k dd) -> p a dd k", dd=d)[:, :, rA:rA + 2, :]
                if gi == 0:
                    P.add("act", lambda e, av=av, pv=pv: e.activation(out=av, in_=pv, func=AF.Copy),
                          reads=[pO], writes=accv)
                else:
                    P.add("dve", lambda e, av=av, pv=pv: e.tensor_tensor(out=av, in0=pv, in1=av, op=ALU.add),
                          reads=[pO] + accv, writes=accv)

            def merge_steps(hd):
                sgB = sgBs[hd % 3]
                anum, aden = acc_parts[hd % 2]
                mhf = mh.rearrange("p t k -> p (t k)")

                def s_recip():
                    P.add("act", lambda e: e.activation(out=aden, in_=aden, func=AF.Ln), reads=[aden], writes=[aden])
                    P.add("act", lambda e: e.activation(out=aden, in_=aden, func=AF.Exp, scale=-1.0),
                          reads=[aden], writes=[aden])

                def s_mul1():
                    P.add("dve", lambda e: e.tensor_tensor(out=anum, in0=anum, in1=aden, op=ALU.mult),
                          reads=[anum, aden], writes=[anum])

                def s_mul2():
                    P.add("dve", lambda e: e.tensor_tensor(out=mhf, in0=anum, in1=sgB, op=ALU.mult),
                          reads=[anum, sgB], writes=[mhf])

                def s_store():
                    dst = mT[TB0:NTA, :, hd, :].rearrange("t p k -> p t k")
                    dma("sp", dst, mh, reads=[mh], dw=[("mTh", TB0 + i, hd) for i in range(NTB)])
                return [s_recip, s_mul1, s_mul2, s_store]

            load_wh(0)
            load_wh(1)
            for g in proj_groups(0):
                g()
            load_kt(0)
            for gi in range(3):
                load_v(0, gi)
            pending = []
            for hd in range(NH):
                if hd + 2 < NH:
                    load_wh(hd + 2)
                pg = proj_groups(hd + 1) if hd + 1 < NH else []
                gp = 0
                base = ucnt[0]
                ucnt[0] += NU

                def sl(j):
                    return (base + j) % 3
                S(hd, 0, sl(0))
                S(hd, 1, sl(1))
                pre(hd, 0, sl(0))
                for j in range(NU):
                    if j + 2 < NU:
                        S(hd, j + 2, sl(j + 2))
                        if j + 2 == NU - 1 and hd + 1 < NH:
                            load_kt(hd + 1)
                    if j + 1 < NU:
                        pre(hd, j + 1, sl(j + 1))
                    k = ((j + 1) * len(pg)) // NU - (j * len(pg)) // NU
                    for _ in range(k):
                        pg[gp]()
                        gp += 1
                    post(hd, j, sl(j))
                    if hd + 1 < NH and j in (7, 15, 23):
                        load_v(hd + 1, j // 8)
                    if pending and j in (1, 3, 5, 9):
                        pending.pop(0)()
                assert gp == len(pg)
                assert not pending
                pending = merge_steps(hd)
            for st in pending:
                st()

        stage_norm0()
        slot = [0]

        def wslot():
            b = WA if slot[0] % 2 == 0 else WB
            slot[0] += 1
            return b

        allA = list(range(NTA))
        allB = list(range(TB0, NTA))
        for g in range(4):
            stage_ag(0, g, wslot(), first_loaded=(g > 0), prefetch_next=(g < 3))
        stage_ao(w_out_a[0], wslot(), x_in, xsA, allA, [(1, hT, "hT", 0)], xkeysrc=None, xkeydst="xsA")
        for g in range(4):
            stage_ag(1, g, wslot(), first_loaded=(g > 0), prefetch_next=(g < 3))
        stage_ao(w_out_a[1], wslot(), xsA, xsB, allA, [(2, hTkv, "hTkv", 0), (3, hT, "hT", TB0)],
                 xkeysrc="xsA", xkeydst="xsB")
        stage_k(WA)
        stage_v(WB, WA + 48 * KB)
        stage_bh(0)
        stage_ao(w_out_b[0], WB, xsB, xsA, allB, [(4, hT, "hT", TB0)], xkeysrc="xsB", xkeydst="xsA")
        stage_bh(1)
        stage_ao(w_out_b[1], WB, xsA, None, allB, [], final=True, xkeysrc="xsA")

        engs = {"pe": nc.tensor, "act": nc.scalar, "dve": nc.vector, "pool": nc.gpsimd, "sp": nc.sync}
        import contextlib
        with contextlib.ExitStack() as es:
            sems = {k: es.enter_context(nc.semaphore("s_" + k)) for k in ("pe", "act", "dve", "pool")}
            dma_sems = {q: [es.enter_context(nc.semaphore("d_%s_%d" % (q, i))) for i in range(NDMASEM)]
                        for q in ("sp", "pool")}
            P.analyze(sems, dma_sems)
            block = es.enter_context(nc.Block())

            @block.tensor
            def _(e):
                P.emit("pe", e)

            @block.scalar
            def _(e):
                P.emit("act", e)

            @block.vector
            def _(e):
                P.emit("dve", e)

            @block.gpsimd
            def _(e):
                P.emit("pool", e)

            @block.sync
            def _(e):
                P.emit("sp", e)
                for s, c in P.final_dma.values():
                    e.wait_ge(s, c)
    return nc, len(P.ops)


_CACHE = {}


def _rope_tables(pos):
    inv_freq = (1.0 / (10000.0 ** (np.arange(0, HD, 2, dtype=np.float32) / np.float32(HD)))).astype(np.float32)
    ang = pos.astype(np.float32)[:, None] * inv_freq[None, :]
    c = np.cos(ang).astype(np.float32)
    s = np.sin(ang).astype(np.float32)
    cosT = np.concatenate([c.T, c.T], axis=0)
    sinT = np.concatenate([-s.T, s.T], axis=0)
    return np.ascontiguousarray(cosT), np.ascontiguousarray(sinT)


def _fm(vec):
    return np.ascontiguousarray(vec.reshape(16, 128).T)


def make_in_maps(x, norm_a, w_in_a, w_grp_a, scale_a, w_out_a, norm_kv, w_k, w_v, norm_b, w_in_b, w_out_b, norm_f):
    f32 = np.float32
    gains = np.concatenate([_fm(norm_a[0]), _fm(norm_a[1]), _fm(norm_kv), _fm(norm_b[0]), _fm(norm_b[1])], axis=1).astype(f32)
    scale_fm = np.concatenate([_fm(scale_a[0]), _fm(scale_a[1])], axis=1).astype(f32)
    kk = np.arange(128)[:, None]
    qq = np.arange(128)[None, :]
    m_prev = (kk >= qq).astype(f32)
    m_same = (kk <= qq).astype(f32)
    maps = []
    for c in range(8):
        b, h = c // 2, c % 2
        if h == 1:
            xin = np.ascontiguousarray(x[b])
            pos = np.arange(LA)
        else:
            xin = np.concatenate([np.zeros((LA - LB, D), f32), x[b, :LB]], axis=0)
            pos = np.arange(LA) - (LA - LB)
        cosT, sinT = _rope_tables(np.maximum(pos, 0))
        invcnt = np.zeros((128, 2, 4, 16), f32)
        for g, w in enumerate(POOLW):
            real = 1.0 / np.minimum(np.arange(16) + 1, w).astype(f32)
            plain = np.full(16, 1.0 / w, f32)
            invcnt[:, 0, g, :] = real if h == 1 else plain
            invcnt[:, 1, g, :] = plain if h == 1 else real
        flag = 1.0 if h == 1 else 0.0
        masks = np.zeros((128, 3, 512), f32)
        for mi, (fa, fb) in enumerate(((1.0, 1.0), (flag, 1.0), (flag, flag))):
            masks[:, mi, 0:128] = m_prev * fa
            masks[:, mi, 128:256] = m_prev * fb
            masks[:, mi, 256:384] = m_same
            masks[:, mi, 384:512] = m_same
        maps.append({
            "x_in": xin.astype(f32), "w_in_a": w_in_a, "w_grp_a": w_grp_a, "w_out_a": w_out_a,
            "w_k": w_k, "w_v": w_v, "w_in_b": w_in_b, "w_out_b": w_out_b,
            "gains": gains, "gf": np.ascontiguousarray(norm_f.reshape(1, D)).astype(f32),
            "scale_fm": scale_fm, "invcnt": invcnt.reshape(128, -1),
            "cosT": cosT, "sinT": sinT,
            "masks": masks.reshape(128, -1).astype(ml_dtypes.bfloat16),
        })
    return maps


def kernel(x, norm_a, w_in_a, w_grp_a, scale_a, w_out_a, norm_kv, w_k, w_v, norm_b, w_in_b, w_out_b, norm_f):
    args = [np.asarray(a, dtype=np.float32) for a in
            (x, norm_a, w_in_a, w_grp_a, scale_a, w_out_a, norm_kv, w_k, w_v, norm_b, w_in_b, w_out_b, norm_f)]
    if "nc" not in _CACHE:
        _CACHE["nc"] = build_program()[0]
    nc = _CACHE["nc"]
    maps = make_in_maps(*args)
    res = run_bass_kernel_spmd(nc, maps, core_ids=list(range(8)))
    out = np.zeros((4, 4096, D), np.float32)
    for c in range(8):
        b, h = c // 2, c % 2
        out[b, h * LB:(h + 1) * LB] = res.results[c]["out"]
    return out
```
